# Optimizing a Trainium2 kernel written in Bass

```python
import math
import jax, jax.numpy as jnp
from jax import lax
import numpy as np

D_MODEL = 1024
BATCH = 16
SEQ = 2048
DEPTH = 1

N_FOX_HEADS = 8
FOX_HEAD_DIM = 64
FOX_WIDTH = N_FOX_HEADS * FOX_HEAD_DIM
N_DIFF_HEADS = 4
DIFF_QK_DIM = 64
DIFF_V_DIM = 2 * DIFF_QK_DIM
DIFF_QK_WIDTH = N_DIFF_HEADS * 2 * DIFF_QK_DIM
DIFF_WIDTH = N_DIFF_HEADS * DIFF_V_DIM
MIX_WIDTH = FOX_WIDTH + DIFF_WIDTH
IN_SIZES = (FOX_WIDTH, FOX_WIDTH, FOX_WIDTH, N_FOX_HEADS,
            DIFF_QK_WIDTH, DIFF_QK_WIDTH, DIFF_WIDTH)
IN_WIDTH = sum(IN_SIZES)
IN_SPLITS = tuple(int(v) for v in np.cumsum(IN_SIZES)[:-1])
ROPE_THETA = 500000.0
ROPE_DIM = DIFF_QK_DIM // 4
D_FF = 2816
FFN_RESIDUAL_WEIGHT = 0.5
Q_BLOCK = 128
RMS_EPS = 1e-6

kernel_name = "hybrid_fox_diffattn_macaron_layer"


def rms_norm(x, g):
    xf = x.astype(jnp.float32)
    y = xf * lax.rsqrt(jnp.mean(xf * xf, axis=-1, keepdims=True) + RMS_EPS)
    return (y * g.astype(jnp.float32)).astype(x.dtype)


def swiglu(x, w_gate, w_up, w_down):
    return (jax.nn.silu(x @ w_gate) * (x @ w_up)) @ w_down


def to_heads(t, n_heads):
    b, s, _ = t.shape
    return t.reshape(b, s, n_heads, -1).transpose(0, 2, 1, 3)


def from_heads(t):
    b, h, s, d = t.shape
    return t.transpose(0, 2, 1, 3).reshape(b, s, h * d)


def partial_rope(t, pos):
    half = ROPE_DIM // 2
    inv_freq = ROPE_THETA ** (-jnp.arange(0, ROPE_DIM, 2, dtype=jnp.float32) / ROPE_DIM)
    ang = pos.astype(jnp.float32)[:, None] * inv_freq[None, :]
    cos, sin = jnp.cos(ang), jnp.sin(ang)
    tf = t.astype(jnp.float32)
    x1, x2, rest = tf[..., :half], tf[..., half:ROPE_DIM], tf[..., ROPE_DIM:]
    out = jnp.concatenate([x1 * cos - x2 * sin, x2 * cos + x1 * sin, rest], axis=-1)
    return out.astype(t.dtype)


def causal_block_attention(q, k, v, log_decay_cum=None):
    seq = q.shape[2]
    scale = q.shape[-1] ** -0.5
    outs = []
    for i in range(seq // Q_BLOCK):
        q0, q1 = i * Q_BLOCK, (i + 1) * Q_BLOCK
        s = jnp.einsum('bhqd,bhkd->bhqk', q[:, :, q0:q1], k[:, :, :q1]).astype(jnp.float32) * scale
        if log_decay_cum is not None:
            s = s + log_decay_cum[:, :, q0:q1, None] - log_decay_cum[:, :, None, :q1]
        mask = jnp.arange(q1)[None, :] <= jnp.arange(q0, q1)[:, None]
        s = jnp.where(mask, s, jnp.finfo(jnp.float32).min)
        p = jax.nn.softmax(s, axis=-1).astype(v.dtype)
        outs.append(jnp.einsum('bhqk,bhkd->bhqd', p, v[:, :, :q1]))
    return jnp.concatenate(outs, axis=2)


def token_mixer(h, w_in, fox_forget_b, lam_q1, lam_k1, lam_q2, lam_k2, subln_g, w_out, layer):
    b, s, _ = h.shape
    pos = jnp.arange(s)
    proj = h @ w_in
    fq, fk, fv, fgate, dq, dk, dv = jnp.split(proj, IN_SPLITS, axis=-1)

    log_f = jax.nn.log_sigmoid(fgate.astype(jnp.float32) + fox_forget_b.astype(jnp.float32))
    c = jnp.cumsum(log_f, axis=1).transpose(0, 2, 1)
    fox_o = causal_block_attention(to_heads(fq, N_FOX_HEADS), to_heads(fk, N_FOX_HEADS),
                                   to_heads(fv, N_FOX_HEADS), c)
    fox_o = from_heads(fox_o)

    dq = dq.reshape(b, s, N_DIFF_HEADS, 2, DIFF_QK_DIM)
    dk = dk.reshape(b, s, N_DIFF_HEADS, 2, DIFF_QK_DIM)
    hd = lambda t: partial_rope(t.transpose(0, 2, 1, 3), pos)
    q1, q2 = hd(dq[..., 0, :]), hd(dq[..., 1, :])
    k1, k2 = hd(dk[..., 0, :]), hd(dk[..., 1, :])
    v = to_heads(dv, N_DIFF_HEADS)
    lam_init = 0.8 - 0.6 * math.exp(-0.3 * layer)
    lam = (jnp.exp(jnp.sum(lam_q1.astype(jnp.float32) * lam_k1.astype(jnp.float32)))
           - jnp.exp(jnp.sum(lam_q2.astype(jnp.float32) * lam_k2.astype(jnp.float32)))
           + lam_init)
    a1 = causal_block_attention(q1, k1, v)
    a2 = causal_block_attention(q2, k2, v)
    diff_o = a1 - lam.astype(a1.dtype) * a2
    diff_o = rms_norm(diff_o, subln_g) * (1.0 - lam_init)
    diff_o = from_heads(diff_o)

    return jnp.concatenate([fox_o, diff_o], axis=-1) @ w_out


def setup_inputs(seed: int = 0) -> dict:
    key = jax.random.key(seed)
    ks = jax.random.split(key, 24)
    L = DEPTH
    nrm = lambda k, shape, scale: jax.random.normal(k, shape, jnp.float32) * scale
    gain = lambda k, n: 1.0 + 0.02 * jax.random.normal(k, (L, n), jnp.float32)
    return {
        "x": jax.random.normal(ks[0], (BATCH, SEQ, D_MODEL), jnp.float32),
        "ffn1_pre_g": gain(ks[1], D_MODEL),
        "ffn1_post_g": gain(ks[2], D_MODEL),
        "ffn1_w_gate": nrm(ks[3], (L, D_MODEL, D_FF), D_MODEL ** -0.5),
        "ffn1_w_up": nrm(ks[4], (L, D_MODEL, D_FF), D_MODEL ** -0.5),
        "ffn1_w_down": nrm(ks[5], (L, D_FF, D_MODEL), D_FF ** -0.5),
        "mix_pre_g": gain(ks[6], D_MODEL),
        "mix_post_g": gain(ks[7], D_MODEL),
        "w_in": nrm(ks[8], (L, D_MODEL, IN_WIDTH), D_MODEL ** -0.5),
        "fox_forget_b": nrm(ks[9], (L, N_FOX_HEADS), 0.02),
        "diff_lambda_q1": nrm(ks[10], (L, DIFF_QK_DIM), 0.1),
        "diff_lambda_k1": nrm(ks[11], (L, DIFF_QK_DIM), 0.1),
        "diff_lambda_q2": nrm(ks[12], (L, DIFF_QK_DIM), 0.1),
        "diff_lambda_k2": nrm(ks[13], (L, DIFF_QK_DIM), 0.1),
        "diff_subln_g": gain(ks[14], DIFF_V_DIM),
        "w_out": nrm(ks[15], (L, MIX_WIDTH, D_MODEL), MIX_WIDTH ** -0.5),
        "ffn2_pre_g": gain(ks[16], D_MODEL),
        "ffn2_post_g": gain(ks[17], D_MODEL),
        "ffn2_w_gate": nrm(ks[18], (L, D_MODEL, D_FF), D_MODEL ** -0.5),
        "ffn2_w_up": nrm(ks[19], (L, D_MODEL, D_FF), D_MODEL ** -0.5),
        "ffn2_w_down": nrm(ks[20], (L, D_FF, D_MODEL), D_FF ** -0.5),
    }


def reference(x, ffn1_pre_g, ffn1_post_g, ffn1_w_gate, ffn1_w_up, ffn1_w_down,
              mix_pre_g, mix_post_g, w_in, fox_forget_b,
              diff_lambda_q1, diff_lambda_k1, diff_lambda_q2, diff_lambda_k2, diff_subln_g,
              w_out, ffn2_pre_g, ffn2_post_g, ffn2_w_gate, ffn2_w_up, ffn2_w_down):
    for l in range(DEPTH):
        f1 = swiglu(rms_norm(x, ffn1_pre_g[l]), ffn1_w_gate[l], ffn1_w_up[l], ffn1_w_down[l])
        x = x + FFN_RESIDUAL_WEIGHT * rms_norm(f1, ffn1_post_g[l])
        m = token_mixer(rms_norm(x, mix_pre_g[l]), w_in[l], fox_forget_b[l],
                        diff_lambda_q1[l], diff_lambda_k1[l], diff_lambda_q2[l], diff_lambda_k2[l],
                        diff_subln_g[l], w_out[l], l)
        x = x + rms_norm(m, mix_post_g[l])
        f2 = swiglu(rms_norm(x, ffn2_pre_g[l]), ffn2_w_gate[l], ffn2_w_up[l], ffn2_w_down[l])
        x = x + FFN_RESIDUAL_WEIGHT * rms_norm(f2, ffn2_post_g[l])
    return x
```

```python
import math
import os
import numpy as np
import concourse.bass as bass
import concourse.mybir as mybir
from concourse.bass_utils import run_bass_kernel_spmd
from contextlib import ExitStack

F32 = mybir.dt.float32
BF16 = mybir.dt.bfloat16
AF = mybir.ActivationFunctionType
ALU = mybir.AluOpType
AX = mybir.AxisListType

P = 128
D = 1024
DFF = 2816
NCH = DFF // P
S = 2048
NT = S // P
G = 512
NG = S // G
TPG = G // P
NSEQ = 2
INW = 3080
EPS = 1e-6
LAM_INIT = 0.8 - 0.6 * math.exp(-0.3 * 0)
NEG = -30000.0
NR = 3
LVL = int(os.environ.get('MIXLVL', '9'))

ENGS = ["pe", "act", "dve", "pool", "sp"]
BLK = {"pe": "tensor", "act": "scalar", "dve": "vector", "pool": "gpsimd", "sp": "sync"}


class Res:
    __slots__ = ("name", "w", "rd")

    def __init__(self, name):
        self.name = name
        self.w = None
        self.rd = []


class Op:
    __slots__ = ("eng", "fn", "deps", "dma", "need", "ev")


class Prog:
    def __init__(self):
        self.q = {e: [] for e in ENGS}
        self.dma_cnt = {}

    def op(self, eng, fn, r=(), w=(), dma=None):
        o = Op()
        o.eng, o.fn, o.dma, o.need, o.ev = eng, fn, dma, False, None
        deps = []
        for x in r:
            if x.w is not None:
                deps.append(x.w)
        for x in w:
            if x.w is not None:
                deps.append(x.w)
            deps.extend(x.rd)
        for x in r:
            x.rd.append(o)
        for x in w:
            x.w = o
            x.rd = []
        seen = set()
        dd = []
        for d in deps:
            if d is o or id(d) in seen:
                continue
            seen.add(id(d))
            if d.eng == "pe" and eng == "pe" and d.dma is None:
                continue
            dd.append(d)
            d.need = True
        o.deps = dd
        if dma is not None:
            self.dma_cnt[dma] = self.dma_cnt.get(dma, 0) + 1
            o.ev = (dma, 16 * self.dma_cnt[dma])
        self.q[eng].append(o)
        return o

    def fence(self, old, new):
        pend = []
        for x in old:
            if x.w is not None:
                pend.append(x.w)
            pend.extend(x.rd)
        for x in new:
            x.w = None
            x.rd = list(pend)

    def finalize(self):
        for e in ENGS:
            c = 0
            for o in self.q[e]:
                if o.dma is None and o.need:
                    c += 1
                    o.ev = (e, c)


def build_program(stages=("ffn1", "mix", "ffn2"), dbg=False):
    nc = bass.Bass("TRN2", target_bir_lowering=False)
    dt = lambda name, shape, d=F32, kind="ExternalInput": nc.dram_tensor(name, shape, d, kind=kind).ap()
    x_d = dt("x", [NSEQ * S, D])
    out_d = dt("out", [NSEQ * S, D], kind="ExternalOutput")
    wg_d = [dt("wg1", [D, DFF]), dt("wg2", [D, DFF])]
    wu_d = [dt("wu1", [D, DFF]), dt("wu2", [D, DFF])]
    wd_d = [dt("wd1", [DFF, D]), dt("wd2", [DFF, D])]
    win_d = dt("w_in", [D, INW])
    wout_d = dt("w_out", [D, D])
    gains_d = dt("gains", [6, D])
    fb_d = dt("fb", [1, 8])
    lam_d = dt("lamv", [4, 64])
    sub_d = dt("subg", [1, 128])
    cs_d = dt("cossin", [2, S, 64])
    cm_d = dt("cmats", [2, P, P])

    es = ExitStack()
    with es:
        tot = nc.sbuf_bytes_remaining // P if False else 208000
        big = es.enter_context(nc.sbuf_tensor("big", [P, tot // 4], F32))
        B0 = nc.lookup_mloc(big).addr
        cur = [0]
        hi = [0]

        def alloc(name, shape, d, at=None):
            nbytes = int(np.prod(shape[1:])) * (4 if d == F32 else 2)
            nbytes = (nbytes + 31) // 32 * 32
            off = cur[0] if at is None else at
            t = nc.alloc_sbuf_tensor_at(name, list(shape), d, offset=B0 + off)
            if at is None:
                cur[0] += nbytes
            hi[0] = max(hi[0], off + nbytes)
            assert hi[0] <= tot, (name, hi[0], tot)
            return t

        X = alloc("X", [P, NT, D], F32)
        gpre = alloc("gpre", [P, D], F32)
        gpost = alloc("gpost", [P, D], F32)
        ring = [alloc(f"ring{i}", [P, 8, 512], BF16) for i in range(NR)]
        htok = [alloc(f"htok{i}", [P, D], BF16) for i in range(2)]
        junk = alloc("junk", [P, D], BF16)
        junk2 = alloc("junk2", [P, D], BF16)
        hT = alloc("hT", [P, 8, G], BF16)
        NPB = 4
        Pball = alloc("Pball", [P, NPB, G], BF16)
        Pb = [Pball[:, i, :] for i in range(NPB)]
        stat = alloc("stat", [P, 64], F32)
        ident = alloc("ident", [P, P], BF16)
        maskb = alloc("maskb", [P, P], BF16)
        onesb = alloc("onesb", [P, P], BF16)
        lamt = alloc("lamt", [P, 4, 64], F32)
        lsm = alloc("lsm", [P, 16], F32)
        fbt = alloc("fbt", [P, 4], F32)
        wgate = alloc("wgate", [P, 8, 8], BF16)
        rtmp = [alloc(f"rtmp{i}", [P, G], F32) for i in range(2)]
        epsb = alloc("epsb", [P, 2], F32)
        ovl = cur[0]
        aT = alloc("aT", [P, NCH, G], BF16)
        Wd = alloc("Wd", [P, NCH, D], BF16)
        silu = [alloc(f"silu{i}", [P, G], BF16) for i in range(2)]
        ffn_end = cur[0]
        cur[0] = ovl
        oTf = alloc("oTf", [P, 4, S], BF16)
        ab = cur[0]
        kx = [alloc(f"kx{h}", [P, S], BF16) for h in range(8)]
        qx = [alloc(f"qx{h}", [P, G], BF16) for h in range(8)]
        Vf = alloc("Vf", [P, NT, 4, 192], BF16)
        kx0 = ab
        cf = [alloc("cf0", [P, G], F32, at=kx0), alloc("cf1", [P, G], F32, at=kx0 + 2048)]
        cst = alloc("cst", [P, 6, G], BF16, at=kx0 + 4096)
        a_end = cur[0]
        cur[0] = ab
        kdT = alloc("kdT", [P, 4, S], BF16)
        qdT = alloc("qdT", [P, 4, G], BF16)
        Vd = alloc("Vd", [P, NT, 512], BF16)
        oTd = alloc("oTd", [P, 4, G], BF16)
        qkb = [alloc(f"qkb{i}", [P, 1024], BF16) for i in range(2)]
        rp = [alloc(f"rp{i}", [P, 8, 8], F32) for i in range(4)]
        dtp_off = cur[0]
        dtp = [alloc(f"dtp{i}", [P, G], F32) for i in range(4)]
        sqb = alloc("sqb", [P, G], BF16)
        qkf = alloc("qkf", [P, 1024], F32, at=dtp_off)
        cosT = alloc("cosT", [P, NT, 64], F32)
        sinT = alloc("sinT", [P, NT, 64], F32)
        b_end = cur[0]
        print("SBUF bytes/partition: common", ovl, "ffn", ffn_end, "mixA", a_end, "mixB", b_end, "limit", tot)

        PSall = es.enter_context(nc.psum_tensor("psall", [P, 8, 512], F32))
        PSallb = PSall.bitcast(BF16)
        PSt = [PSall[:, i, :] for i in range(8)]
        PSb = [PSallb[:, i, :] for i in range(8)]

        R = Res
        rX = [R(f"X{i}") for i in range(NT)]
        rgpre, rgpost = R("gpre"), R("gpost")
        rring = [[R(f"ring{i}a"), R(f"ring{i}b")] for i in range(NR)]
        rhtok = [R("htok0"), R("htok1")]
        rjunk = R("junk")
        rhT = [R(f"hT{i}") for i in range(TPG)]
        rPb = [R(f"Pb{i}") for i in range(NPB)]
        rstat = {}
        rconst = R("const")
        rlam = R("lam")
        rwgate = R("wgate")
        rrtmp = [R("rtmp0"), R("rtmp1")]
        rPS = [R(f"ps{i}") for i in range(8)]
        raT = [R(f"aT{c}") for c in range(NCH)]
        rWd = R("Wd")
        rWdp = [R(f"Wd{c}") for c in range(NCH // 2)]
        rsilu = [R("silu0"), R("silu1")]
        FFN_RES = raT + [rWd] + rWdp + rsilu
        roTf = [R(f"oTf{q}") for q in range(NG)]
        rkx = [[R(f"kx{h}_{q}") for q in range(NG)] for h in range(8)]
        rkxa = [[[R(f"kxa{h}_{q}_{i}") for i in range(3)] for q in range(NG)] for h in range(8)]
        rqx = [R(f"qx{h}") for h in range(8)]
        rqxa = [[R(f"qxa{h}_{i}") for i in range(3)] for h in range(8)]
        rkqinit = R("kqinit")
        rVf = [R(f"Vf{t}") for t in range(NT)]
        rVfones = R("Vfones")
        rcst = R("cst")
        rcf = [R("cf0"), R("cf1")]
        rcarry = R("carry")
        A_RES = [x for l in rkx for x in l] + [x for l in rkxa for m_ in l for x in m_] + rqx + [x for l in rqxa for x in l] + [rkqinit] + rVf + [rVfones, rcst] + rcf
        rkdT = [[R(f"kdT{h}_{t}") for t in range(NT)] for h in range(4)]
        rqdT = [[R(f"qdT{h}_{i}") for i in range(TPG)] for h in range(4)]
        rVd = [R(f"Vd{t}") for t in range(NT)]
        roTd = [R(f"oTd{h}") for h in range(4)]
        rqkb = [R("qkb0"), R("qkb1")]
        rrp = [R(f"rp{i}") for i in range(4)]
        rdtp = [R(f"dtp{i}") for i in range(4)]
        rsqb = R("sqb")
        rcsx = R("csx")
        B_RES = [x for l in rkdT for x in l] + [x for l in rqdT for x in l] + rVd + roTd + rqkb + rrp + rdtp + [rsqb, rcsx]
        MIX_RES = roTf

        pg = Prog()
        sem_names = set()
        rcarry0 = R("carry0")

        def dma(eng, out, in_, sem, r=(), w=()):
            sem_names.add(sem)
            return pg.op(eng, lambda e: e.dma_start(out=out, in_=in_), r=r, w=w, dma=sem)

        fills = []
        fill_pos = [0]
        issued = [0]

        def plan_fills():
            def ffn_f(which):
                gv = wg_d[which].rearrange("(k p) n -> p k n", p=P)
                uv = wu_d[which].rearrange("(k p) n -> p k n", p=P)
                for q in range(NG):
                    for cp in range(NCH // 2):
                        fills.append([(0, 256, gv[:, :, cp * 256:(cp + 1) * 256], 0), (256, 512, uv[:, :, cp * 256:(cp + 1) * 256], 1)])
            wv = win_d.rearrange("(k p) n -> p k n", p=P)
            ov = wout_d.rearrange("(k p) n -> p k n", p=P)
            for s in range(NSEQ):
                if "ffn1" in stages:
                    ffn_f(0)
                if "mix" in stages:
                    for q in range(NG):
                        for c0 in (0, 512, 1024):
                            fills.append([(0, 512, wv[:, :, c0:c0 + 512], 0)])
                    for q in range(NG):
                        for c0 in (1544, 2056, 2568):
                            fills.append([(0, 512, wv[:, :, c0:c0 + 512], 0)])
                        for c0 in (0, 512):
                            fills.append([(0, 512, ov[:, :, c0:c0 + 512], 0)])
                if "ffn2" in stages:
                    ffn_f(1)

        def ensure(n):
            while issued[0] <= n and issued[0] < len(fills):
                i = issued[0]
                sl = i % NR
                for (c0, c1, src, part) in fills[i]:
                    ws = [rring[sl][part]] if len(fills[i]) == 2 else rring[sl]
                    dma("pool", ring[sl][:, :, c0:c1], src, f"ring{sl}_{part}", w=ws)
                issued[0] += 1

        def next_fill(look=NR - 1):
            n = fill_pos[0]
            fill_pos[0] += 1
            ensure(n + look)
            return n % NR

        plan_fills()

        stg = rtmp[0]
        dma("sp", stg[:, 0:128], cm_d[0], "c0", w=[rrtmp[0]])
        dma("sp", stg[:, 128:256], cm_d[1], "c1", w=[rrtmp[0]])
        for i in range(4):
            dma("sp", lamt[:, i, :], lam_d[i:i + 1, :].to_broadcast([P, 64]), f"c4_{i}", w=[rlam])
        dma("sp", lsm[:, 8:9], sub_d.rearrange("o d -> d o"), "c5", w=[rlam])
        dma("sp", fbt[96:104, 0:1], fb_d.rearrange("o h -> h o"), "c6", w=[rlam])
        pg.op("dve", lambda e: e.memset(fbt[96:104, 2:3], 1.0), w=[rcarry0])
        pg.op("dve", lambda e: e.tensor_scalar(out=fbt[96:104, 3:4], in0=fbt[96:104, 0:1], scalar1=-1.0, scalar2=None, op0=ALU.mult), r=[rlam], w=[rlam])
        pg.op("dve", lambda e: e.tensor_copy(out=ident[:, :], in_=stg[:, 0:128]), r=[rrtmp[0]], w=[rconst])
        pg.op("dve", lambda e: e.tensor_copy(out=maskb[:, :], in_=stg[:, 128:256]), r=[rrtmp[0]], w=[rconst])
        pg.op("dve", lambda e: e.memset(onesb[:, :], 1.0), w=[rconst])
        pg.op("dve", lambda e: e.tensor_tensor(out=lamt[:, 0, :], in0=lamt[:, 0, :], in1=lamt[:, 1, :], op=ALU.mult), r=[rlam], w=[rlam])
        pg.op("dve", lambda e: e.tensor_tensor(out=lamt[:, 2, :], in0=lamt[:, 2, :], in1=lamt[:, 3, :], op=ALU.mult), r=[rlam], w=[rlam])
        pg.op("dve", lambda e: e.reduce_sum(out=lsm[:, 0:1], in_=lamt[:, 0, :], axis=AX.X), r=[rlam], w=[rlam])
        pg.op("dve", lambda e: e.reduce_sum(out=lsm[:, 1:2], in_=lamt[:, 2, :], axis=AX.X), r=[rlam], w=[rlam])
        pg.op("act", lambda e: e.activation(out=lsm[:, 2:4], in_=lsm[:, 0:2], func=AF.Exp), r=[rlam], w=[rlam])
        pg.op("dve", lambda e: e.scalar_tensor_tensor(out=lsm[:, 4:5], in0=lsm[:, 3:4], scalar=-LAM_INIT, in1=lsm[:, 2:3], op0=ALU.add, op1=ALU.subtract), r=[rlam], w=[rlam])
        pg.op("dve", lambda e: e.tensor_scalar(out=lsm[:, 9:10], in0=lsm[:, 8:9], scalar1=1.0 - LAM_INIT, scalar2=None, op0=ALU.mult), r=[rlam], w=[rlam])
        neglam = lsm[:, 4:5]
        gsub = lsm[:, 9:10]

        stat_i = [0]

        def stat_cols(n):
            c = stat_i[0]
            if c + n > 64:
                c = 0
            stat_i[0] = c + n
            key = (c, n)
            if key not in rstat:
                rstat[key] = R(f"stat{c}_{n}")
            return c, rstat[key]

        rstatN = [R("statN0"), R("statN1")]
        rstatP = [R("statP0"), R("statP1")]
        ngc = [0]
        puc = [0]

        gpre_cur = [None]

        def load_gpre(ipre):
            if gpre_cur[0] != ipre:
                dma("sp", gpre[:, :], gains_d[ipre:ipre + 1, :].to_broadcast([P, D]), "gpre", w=[rgpre])
                gpre_cur[0] = ipre

        def load_gains(ipre, ipost):
            load_gpre(ipre)
            dma("sp", gpost[:, :], gains_d[ipost:ipost + 1, :].to_broadcast([P, D]), "gpost", w=[rgpost])

        phases = []
        if tuple(stages) == ("ffn1", "mix", "ffn2"):
            for sq in range(NSEQ):
                phases += [(("ffn", sq, 0, 0), 0), (("A", sq, 0), 2), (("B", sq, 0), 2), (("ffn", sq, 1, 0), 4)]
        phase_i = [-1]

        def xphase_pre():
            i = phase_i[0] + 1
            if not phases or i >= len(phases):
                return
            load_gpre(phases[i][1])
            norm_pre(0)

        def xphase_post(tbanks):
            i = phase_i[0] + 1
            if not phases or i >= len(phases):
                return
            norm_post(0, tbanks)
            ndone.add(phases[i][0])

        tb = [0]

        npre = {}
        ndone = set()

        def norm_pre(q):
            par = ngc[0] % 2
            ngc[0] += 1
            c0 = 32 * par
            rstat_all = rstatN[par]
            npre[q] = (c0, rstat_all)
            for i in range(TPG):
                ti = q * TPG + i
                pg.op("dve", lambda e, ti=ti, i=i: e.scalar_tensor_tensor(out=junk2[:, :], in0=X[:, ti, :], scalar=1.0, in1=X[:, ti, :], op0=ALU.mult, op1=ALU.mult, accum_out=stat[:, c0 + i:c0 + i + 1]),
                      r=[rX[ti]], w=[rstat_all])
            pg.op("act", lambda e: e.activation(out=stat[:, c0 + 4:c0 + 8], in_=stat[:, c0:c0 + 4], func=AF.Ln, scale=1.0 / D, bias=epsb[:, 0:1]), r=[rstat_all], w=[rstat_all])
            pg.op("act", lambda e: e.activation(out=stat[:, c0 + 8:c0 + 12], in_=stat[:, c0 + 4:c0 + 8], func=AF.Exp, scale=-0.5), r=[rstat_all], w=[rstat_all])

        def norm_post(q, tbanks=(6, 7)):
            c0, rstat_all = npre.pop(q)
            for i in range(TPG):
                ti = q * TPG + i
                b = tb[0] % 2
                tb[0] += 1
                pg.op("dve", lambda e, ti=ti, i=i, b=b: e.scalar_tensor_tensor(out=htok[b][:, :], in0=X[:, ti, :], scalar=stat[:, c0 + 8 + i:c0 + 9 + i], in1=gpre[:, :], op0=ALU.mult, op1=ALU.mult),
                      r=[rX[ti], rstat_all, rgpre], w=[rhtok[b]])
                pb = tbanks[b]

                def tr(e, b=b, pb=pb):
                    ins = None
                    for k in range(8):
                        ins = e.transpose(out=PSb[pb][:, k * P:(k + 1) * P], in_=htok[b][:, k * P:(k + 1) * P], identity=ident[:, :])
                    return ins
                pg.op("pe", tr, r=[rhtok[b], rconst], w=[rPS[pb]])
                pg.op("act", lambda e, i=i, pb=pb: e.activation(out=hT[:, :, i * P:(i + 1) * P], in_=PSb[pb][:, :].rearrange("p (k t) -> p k t", k=8), func=AF.Copy),
                      r=[rPS[pb]], w=[rhT[i]])

        def norm_group(q, key=None, tbanks=(6, 7)):
            if key is not None and key in ndone:
                return
            norm_pre(q)
            norm_post(q, tbanks)
            while late_x:
                late_x.pop(0)()

        def post_update(ti, banks, half_scale, store_seq=None):
            par = puc[0] % 2
            puc[0] += 1
            c0 = 16 + 32 * par
            rstat_all = rstatP[par]
            for nh in range(2):
                pg.op("act", lambda e, nh=nh: e.activation(out=junk[:, nh * 512:(nh + 1) * 512], in_=PSt[banks[nh]][:, :], func=AF.Square, accum_out=stat[:, c0 + nh:c0 + nh + 1]),
                      r=[rPS[banks[nh]]], w=[rstat_all])
            pg.op("dve", lambda e: e.tensor_tensor(out=stat[:, c0 + 2:c0 + 3], in0=stat[:, c0:c0 + 1], in1=stat[:, c0 + 1:c0 + 2], op=ALU.add), r=[rstat_all], w=[rstat_all])
            sc = (1.0 / D) / (half_scale * half_scale)
            bt = epsb[:, 1:2] if half_scale == 0.5 else epsb[:, 0:1]
            pg.op("act", lambda e: e.activation(out=stat[:, c0 + 3:c0 + 4], in_=stat[:, c0 + 2:c0 + 3], func=AF.Sqrt, scale=sc, bias=bt), r=[rstat_all], w=[rstat_all])
            pg.op("dve", lambda e: e.reciprocal(out=stat[:, c0 + 4:c0 + 5], in_=stat[:, c0 + 3:c0 + 4]), r=[rstat_all], w=[rstat_all])
            for nh in range(2):
                pg.op("dve", lambda e, nh=nh: e.scalar_tensor_tensor(out=rtmp[nh][:, :], in0=PSt[banks[nh]][:, :], scalar=stat[:, c0 + 4:c0 + 5], in1=gpost[:, nh * 512:(nh + 1) * 512], op0=ALU.mult, op1=ALU.mult),
                      r=[rPS[banks[nh]], rstat_all, rgpost], w=[rrtmp[nh]])
                pg.op("dve", lambda e, nh=nh: e.tensor_tensor(out=X[:, ti, nh * 512:(nh + 1) * 512], in0=X[:, ti, nh * 512:(nh + 1) * 512], in1=rtmp[nh][:, :], op=ALU.add),
                      r=[rrtmp[nh], rX[ti]], w=[rX[ti]])
            if store_seq is not None:
                row = store_seq * S + ti * P
                dma("sp", out_d[row:row + P, :], X[:, ti, :], f"xs{ti}", r=[rX[ti]])
                if store_seq + 1 < NSEQ:
                    row = (store_seq + 1) * S + ti * P
                    dma("sp", X[:, ti, :], x_d[row:row + P, :], f"xs{ti}", w=[rX[ti]])
                    preloaded.add((store_seq + 1, ti))

        repsb = R("epsb")
        pg.op("dve", lambda e: e.memset(epsb[:, 0:1], EPS), w=[repsb])
        pg.op("dve", lambda e: e.memset(epsb[:, 1:2], 4.0 * EPS), w=[repsb])
        rconst_eps = repsb

        gub = [0]
        dbk = [0]
        preloaded = set()
        late_x = []
        ovl_user = [None]
        OVL = {"ffn": FFN_RES, "A": A_RES + MIX_RES, "B": B_RES + MIX_RES}

        def enter_overlay(name):
            old = ovl_user[0]
            if old == name:
                return
            if old is not None:
                if old == "A" and name == "B":
                    pg.fence(A_RES, B_RES)
                else:
                    new = OVL[name]
                    pg.fence(OVL[old], new)
            ovl_user[0] = name

        def ffn_phase(seq, which, last):
            phase_i[0] += 1
            enter_overlay("ffn")
            load_gains(0 if which == 0 else 4, 1 if which == 0 else 5)
            wdv = wd_d[which].rearrange("(c p) n -> p c n", p=P)
            first = [True]
            for q in range(NG):
                norm_group(q, key=("ffn", seq, which, q), tbanks=(0, 1))
                for cp in range(NCH // 2):
                    sl = next_fill()
                    if q == 0:
                        dma("pool", Wd[:, 2 * cp:2 * cp + 2, :], wdv[:, 2 * cp:2 * cp + 2, :], f"wd{cp % 2}", w=[rWdp[cp]])
                    for cc in range(2):
                        c = cp * 2 + cc
                        gb = (gub[0] % 2) * 2
                        gub[0] += 1
                        ub = gb + 1

                        def mm(e, sl=sl, off=0, cc=cc, bank=gb):
                            ins = None
                            for k in range(8):
                                ins = e.matmul(PSt[bank][:, :], lhsT=ring[sl][:, k, off + cc * P: off + (cc + 1) * P], rhs=hT[:, k, :], start=(k == 0), stop=(k == 7))
                            return ins
                        pg.op("pe", lambda e, sl=sl, cc=cc, gb=gb: mm(e, sl, 0, cc, gb), r=[rring[sl][0]] + rhT, w=[rPS[gb]])
                        pg.op("pe", lambda e, sl=sl, cc=cc, ub=ub: mm(e, sl, 256, cc, ub), r=[rring[sl][1]] + rhT, w=[rPS[ub]])
                        sb = c % 2
                        pg.op("act", lambda e, gb=gb, sb=sb: e.activation(out=silu[sb][:, :], in_=PSt[gb][:, :], func=AF.Silu), r=[rPS[gb]], w=[rsilu[sb]])
                        pg.op("dve", lambda e, c=c, ub=ub, sb=sb: e.tensor_tensor(out=aT[:, c, :], in0=silu[sb][:, :], in1=PSt[ub][:, :], op=ALU.mult), r=[rsilu[sb], rPS[ub]], w=[raT[c]])
                for i in range(TPG):
                    ti = q * TPG + i
                    d0 = 4 + (dbk[0] % 2) * 2
                    dbk[0] += 1

                    def dn(e, i=i, d0=d0):
                        ins = None
                        for c in range(NCH):
                            for nh in range(2):
                                ins = e.matmul(PSt[d0 + nh][:, :], lhsT=aT[:, c, i * P:(i + 1) * P], rhs=Wd[:, c, nh * 512:(nh + 1) * 512], start=(c == 0), stop=(c == NCH - 1))
                        return ins
                    pg.op("pe", dn, r=raT + rWdp, w=[rPS[d0], rPS[d0 + 1]])
                    if q + 1 < NG and i == 0:
                        norm_pre(q + 1)
                    if q + 1 < NG and i == 1:
                        norm_post(q + 1, tbanks=(0, 1))
                        ndone.add(("ffn", seq, which, q + 1))
                    if q == NG - 1 and i == 0:
                        xphase_pre()
                    if q == NG - 1 and i == 1:
                        xphase_post((0, 1))
                    post_update(ti, (d0, d0 + 1), 0.5, store_seq=(seq if last else None))

        def evac(eng, out, in_, r, w, scale=None):
            if eng == "act":
                if scale is None:
                    return pg.op("act", lambda e: e.activation(out=out, in_=in_, func=AF.Copy), r=r, w=w)
                return pg.op("act", lambda e: e.activation(out=out, in_=in_, func=AF.Copy, scale=scale), r=r, w=w)
            if scale is None:
                return pg.op("dve", lambda e: e.tensor_copy(out=out, in_=in_), r=r, w=w)
            return pg.op("dve", lambda e: e.tensor_scalar(out=out, in0=in_, scalar1=scale, scalar2=None, op0=ALU.mult), r=r, w=w)

        pjb = [0]
        dsb = [0]
        dpb = [0]
        sbk = [0]
        pbk = [0]

        def mix_a(seq):
            phase_i[0] += 1
            enter_overlay("A")
            load_gains(2, 3)
            dma("pool", wgate[:, :, :], win_d.rearrange("(k p) n -> p k n", p=P)[:, :, 1536:1544], "wgate", w=[rwgate])
            for h in range(8):
                pg.op("pool", lambda e, h=h: e.memset(kx[h][64:70, :], 1.0), w=[rkqinit])
                pg.op("pool", lambda e, h=h: e.memset(qx[h][64:70, :], 1.0), w=[rkqinit])
            pg.op("pool", lambda e: e.memset(Vf[:, :, :, 64:128], 1.0), w=[rVfones])
            for q in range(NG):
                norm_group(q, key=("A", seq, q))
                tok = slice(q * G, (q + 1) * G)
                def gate_part1(qq, bank):
                    tk = slice(qq * G, (qq + 1) * G)

                    def mmg(e, bank=bank):
                        ins = None
                        for k in range(8):
                            ins = e.matmul(PSt[bank][0:8, :], lhsT=wgate[:, k, :], rhs=hT[:, k, :], start=(k == 0), stop=(k == 7))
                        return ins
                    pg.op("pe", mmg, r=[rwgate] + rhT, w=[rPS[bank]])
                    pg.op("act", lambda e, bank=bank: e.activation(out=cf[0][96:104, :], in_=PSt[bank][0:8, :], func=AF.Exp, scale=-1.0, bias=fbt[96:104, 3:4]), r=[rPS[bank], rlam], w=[rcf[0]])
                    pg.op("act", lambda e: e.activation(out=cf[0][96:104, :], in_=cf[0][96:104, :], func=AF.Ln, bias=fbt[96:104, 2:3]), r=[rcf[0], rcarry0], w=[rcf[0]])
                    ini = 0.0 if qq == 0 else fbt[96:104, 1:2]
                    rr = [rcf[0], rcarry0] + ([] if qq == 0 else [rcarry])
                    pg.op("dve", lambda e: e.tensor_tensor_scan(out=cf[1][96:104, :], data0=fbt[96:104, 2:3].to_broadcast([8, G]), data1=cf[0][96:104, :], initial=ini, op0=ALU.mult, op1=ALU.subtract), r=rr, w=[rcf[1]])
                    pg.op("dve", lambda e: e.tensor_copy(out=fbt[96:104, 1:2], in_=cf[1][96:104, G - 1:G]), r=[rcf[1]], w=[rcarry])
                    pg.op("pool", lambda e: e.tensor_copy(out=cst[96:104, 0, :], in_=cf[1][96:104, :]), r=[rcf[1]], w=[rcst])
                    pg.op("pool", lambda e: e.tensor_tensor(out=cf[0][96:104, :], in0=cf[1][96:104, :], in1=cst[96:104, 0, :], op=ALU.subtract), r=[rcf[1], rcst], w=[rcf[0]])
                    pg.op("pool", lambda e: e.tensor_copy(out=cst[96:104, 1, :], in_=cf[0][96:104, :]), r=[rcf[0]], w=[rcst])
                    pg.op("pool", lambda e: e.tensor_tensor(out=cf[0][96:104, :], in0=cf[0][96:104, :], in1=cst[96:104, 1, :], op=ALU.subtract), r=[rcf[0], rcst], w=[rcf[0]])
                    pg.op("pool", lambda e: e.tensor_copy(out=cst[96:104, 2, :], in_=cf[0][96:104, :]), r=[rcf[0]], w=[rcst])
                    pg.op("pool", lambda e: e.tensor_scalar(out=cst[96:104, 3:6, :], in0=cst[96:104, 0:3, :], scalar1=-1.0, scalar2=1.0, op0=ALU.mult, op1=ALU.mult), r=[rcst], w=[rcst])
                    for h in range(8):
                        for i3 in range(3):
                            dma("sp", kx[h][67 + i3:68 + i3, tk], cst[96 + h:97 + h, 3 + i3, :], f"ka{h}_{i3}", r=[rcst, rkqinit], w=[rkxa[h][qq][i3]])

                def gate_part2():
                    for h in range(8):
                        for i3 in range(3):
                            dma("sp", qx[h][64 + i3:65 + i3, :], cst[96 + h:97 + h, i3, :], f"qa{h}_{i3}", r=[rcst, rkqinit], w=[rqxa[h][i3]])

                if q == 0:
                    bank = pjb[0] % 2
                    pjb[0] += 1
                    gate_part1(0, bank)
                gate_part2()
                for which_qk in range(2):
                    sl = next_fill()
                    for m in range(4):
                        bank = pjb[0] % 2
                        pjb[0] += 1

                        def mm(e, sl=sl, m=m, bank=bank):
                            ins = None
                            for k in range(8):
                                ins = e.matmul(PSt[bank][:, :], lhsT=ring[sl][:, k, m * P:(m + 1) * P], rhs=hT[:, k, :], start=(k == 0), stop=(k == 7))
                            return ins
                        pg.op("pe", mm, r=rring[sl] + rhT, w=[rPS[bank]])
                        for par in range(2):
                            h = 2 * m + par
                            src = PSt[bank][par * 64:(par + 1) * 64, :]
                            extra = [] if par == 0 else ([rqx[h - 1]] if which_qk == 0 else [rkx[h - 1][q]])
                            if which_qk == 0:
                                evac("act" if par == 0 else "dve", qx[h][0:64, :], src, [rPS[bank]] + extra, [rqx[h]], scale=0.125)
                            else:
                                evac("act" if par == 0 else "dve", kx[h][0:64, tok], src, [rPS[bank]] + extra, [rkx[h][q]])

                sl = next_fill()
                for i in range(TPG):
                    ti = q * TPG + i
                    bank = 2 + pjb[0] % 2
                    pjb[0] += 1

                    def mm(e, sl=sl, i=i, bank=bank):
                        ins = None
                        for k in range(8):
                            ins = e.matmul(PSt[bank][:, :], lhsT=hT[:, k, i * P:(i + 1) * P], rhs=ring[sl][:, k, :], start=(k == 0), stop=(k == 7))
                        return ins
                    pg.op("pe", mm, r=rring[sl] + [rhT[i]], w=[rPS[bank]])
                    v4 = PSt[bank][:, :].rearrange("p (m e d) -> p m e d", m=4, e=2)
                    evac("act", Vf[:, ti, :, 0:64], v4[:, :, 0, :], [rPS[bank]], [rVf[ti]])
                    evac("dve", Vf[:, ti, :, 128:192], v4[:, :, 1, :], [rPS[bank], rVfones], [rVf[ti]])
                if LVL < 3:
                    continue
                nkt = q * TPG + TPG
                LA = 2
                steps = [(h, j) for h in range(8) for j in range(nkt)]
                info = {}
                for si in range(len(steps) + LA):
                    if si < len(steps):
                        h, j = steps[si]
                        n0 = max(j - q * TPG, 0)
                        diag = j >= q * TPG
                        sb = sbk[0] % 3
                        sbk[0] += 1
                        pb = pbk[0] % NPB
                        pbk[0] += 1
                        qs = slice(n0 * P, G)

                        def smm(e, h=h, j=j, n0=n0, diag=diag, sb=sb, qs=qs):
                            if not diag:
                                return e.matmul(PSt[sb][:, qs], lhsT=kx[h][0:70, j * P:(j + 1) * P], rhs=qx[h][0:70, qs], start=True, stop=True)
                            dq_ = slice(n0 * P, (n0 + 1) * P)
                            e.matmul(PSt[sb][:, dq_], lhsT=kx[h][0:70, j * P:(j + 1) * P], rhs=qx[h][0:70, dq_], start=True, stop=False)
                            ins = e.matmul(PSt[sb][:, dq_], lhsT=ident[:, :], rhs=maskb[:, :], start=False, stop=True)
                            if (n0 + 1) * P < G:
                                rq_ = slice((n0 + 1) * P, G)
                                ins = e.matmul(PSt[sb][:, rq_], lhsT=kx[h][0:70, j * P:(j + 1) * P], rhs=qx[h][0:70, rq_], start=True, stop=True, skip_group_check=True)
                            return ins
                        pg.op("pe", smm, r=[rkx[h][j // TPG], rqx[h], rconst] + rkxa[h][j // TPG] + rqxa[h], w=[rPS[sb]])
                        pg.op("act", lambda e, sb=sb, pb=pb, qs=qs: e.activation(out=Pb[pb][:, qs], in_=PSt[sb][:, qs], func=AF.Exp), r=[rPS[sb]], w=[rPb[pb]])
                        info[si] = (pb, qs)
                    if si == len(steps) // 2 and q + 1 < NG:
                        norm_pre(q + 1)
                    if si == len(steps) // 2 + 4 and q + 1 < NG:
                        norm_post(q + 1)
                        ndone.add(("A", seq, q + 1))
                    if si == len(steps) // 2 + 8 and q + 1 < NG:
                        gate_part1(q + 1, 3)
                    if si == len(steps) // 2 and q == NG - 1:
                        xphase_pre()
                    if si == len(steps) // 2 + 4 and q == NG - 1:
                        xphase_post((6, 7))
                    sj = si - LA
                    if sj >= 0:
                        h, jj = steps[sj]
                        pbb, qss = info.pop(sj)
                        ob = 4 + h % 2
                        m, par = h // 2, h % 2
                        vsl = slice(0, 128) if par == 0 else slice(64, 192)
                        pg.op("pe", lambda e, jj=jj, pbb=pbb, qss=qss, m=m, vsl=vsl, ob=ob: e.matmul(PSt[ob][:, qss], lhsT=Vf[:, jj, m, vsl], rhs=Pb[pbb][:, qss], start=(jj == 0), stop=(jj == nkt - 1)),
                              r=[rVf[jj], rVfones, rPb[pbb]], w=[rPS[ob]])
                        if jj == nkt - 1:
                            nrow = slice(0, 64) if par == 0 else slice(64, 128)
                            drow = slice(64, 128) if par == 0 else slice(0, 64)
                            rt = rtmp[h % 2]
                            if q <= 1:
                                pg.op("act", lambda e, ob=ob, drow=drow, rt=rt: e.activation(out=rt[drow, :], in_=PSt[ob][drow, :], func=AF.Ln), r=[rPS[ob]], w=[rrtmp[h % 2]])
                                pg.op("act", lambda e, drow=drow, rt=rt: e.activation(out=rt[drow, :], in_=rt[drow, :], func=AF.Exp, scale=-1.0), r=[rrtmp[h % 2]], w=[rrtmp[h % 2]])
                            else:
                                pg.op("dve", lambda e, ob=ob, drow=drow, rt=rt: e.reciprocal(out=rt[drow, :], in_=PSt[ob][drow, :]), r=[rPS[ob]], w=[rrtmp[h % 2]])
                            pg.op("dve", lambda e, ob=ob, nrow=nrow, drow=drow, rt=rt, m=m, tok=tok: e.tensor_tensor(out=oTf[nrow, m, tok], in0=PSt[ob][nrow, :], in1=rt[drow, :], op=ALU.mult),
                                  r=[rPS[ob], rrtmp[h % 2]], w=[roTf[q]])

        def mix_b(seq):
            phase_i[0] += 1
            enter_overlay("B")
            if LVL < 4:
                return
            dma("sp", cosT[:, :, :], cs_d[0].rearrange("(t p) f -> p t f", p=P), "c2", w=[rcsx])
            dma("sp", sinT[:, :, :], cs_d[1].rearrange("(t p) f -> p t f", p=P), "c3", w=[rcsx])
            for q in range(NG):
                norm_group(q, key=("B", seq, q))
                tok = slice(q * G, (q + 1) * G)
                sl_q = next_fill()
                sl_k = next_fill(NR - 2)
                pend_tr = None
                for i in range(TPG + 1):
                    if i < TPG:
                        ti = q * TPG + i
                        bq = (pjb[0] % 2) * 2
                        pjb[0] += 1
                        bk = bq + 1
                        b2 = i % 2

                        def mm(e, sl, i, bank):
                            ins = None
                            for k in range(8):
                                ins = e.matmul(PSt[bank][:, :], lhsT=hT[:, k, i * P:(i + 1) * P], rhs=ring[sl][:, k, :], start=(k == 0), stop=(k == 7))
                            return ins
                        pg.op("pe", lambda e, i=i, bq=bq, sl_q=sl_q: mm(e, sl_q, i, bq), r=rring[sl_q] + [rhT[i]], w=[rPS[bq]])
                        pg.op("pe", lambda e, i=i, bk=bk, sl_k=sl_k: mm(e, sl_k, i, bk), r=rring[sl_k] + [rhT[i]], w=[rPS[bk]])
                        for half, bank in ((0, bq), (1, bk)):
                            evac("act", qkb[b2][:, half * 512:(half + 1) * 512], PSt[bank][:, :], [rPS[bank]], [rqkb[b2]])
                            evac("dve", qkf[:, half * 512:(half + 1) * 512], PSt[bank][:, :], [rPS[bank], rqkb[b2]], rdtp[0:2], scale=1.0)
                        o3 = qkb[b2][:, :].rearrange("p (m d) -> p m d", m=16)
                        qk3 = qkf[:, :].rearrange("p (m d) -> p m d", m=16)
                        cb = cosT[:, ti, :].rearrange("p (m f) -> p m f", m=8)
                        sn = sinT[:, ti, :].rearrange("p (m f) -> p m f", m=8)
                        for half in range(0 if not os.environ.get("NOROPE") else 2, 2):
                            hs = slice(half * 8, (half + 1) * 8)
                            x1, x2 = qk3[:, hs, 0:8], qk3[:, hs, 8:16]
                            rr = rdtp[0:2] + [rcsx]
                            pg.op("dve", lambda e, x1=x1, cb=cb: e.tensor_tensor(out=rp[0][:, :, :], in0=x1, in1=cb, op=ALU.mult), r=rr, w=[rrp[0]])
                            pg.op("dve", lambda e, x2=x2, sn=sn: e.tensor_tensor(out=rp[1][:, :, :], in0=x2, in1=sn, op=ALU.mult), r=rr, w=[rrp[1]])
                            pg.op("dve", lambda e, o3=o3, hs=hs: e.tensor_tensor(out=o3[:, hs, 0:8], in0=rp[0][:, :, :], in1=rp[1][:, :, :], op=ALU.subtract), r=rrp[0:2], w=[rqkb[b2]])
                            pg.op("dve", lambda e, x2=x2, cb=cb: e.tensor_tensor(out=rp[2][:, :, :], in0=x2, in1=cb, op=ALU.mult), r=rr, w=[rrp[2]])
                            pg.op("dve", lambda e, x1=x1, sn=sn: e.tensor_tensor(out=rp[3][:, :, :], in0=x1, in1=sn, op=ALU.mult), r=rr, w=[rrp[3]])
                            pg.op("dve", lambda e, o3=o3, hs=hs: e.tensor_tensor(out=o3[:, hs, 8:16], in0=rp[2][:, :, :], in1=rp[3][:, :, :], op=ALU.add), r=rrp[2:4], w=[rqkb[b2]])
                        cur_tr = (i, ti, b2)
                    if pend_tr is not None:
                        ii, tii, bb = pend_tr
                        pb = 6 + tb[0] % 2
                        tb[0] += 1

                        def tr(e, pb=pb, bb=bb):
                            ins = None
                            for c in range(8):
                                ins = e.transpose(out=PSb[pb][:, c * P:(c + 1) * P], in_=qkb[bb][:, c * P:(c + 1) * P], identity=ident[:, :])
                            return ins
                        pg.op("pe", tr, r=[rqkb[bb], rconst], w=[rPS[pb]])
                        srcq = PSb[pb][:, 0:512].rearrange("p (h t) -> p h t", h=4)
                        srck = PSb[pb][:, 512:1024].rearrange("p (h t) -> p h t", h=4)
                        evac("act", qdT[:, :, ii * P:(ii + 1) * P], srcq, [rPS[pb]], [rqdT[hh][ii] for hh in range(4)])
                        evac("act", kdT[:, :, tii * P:(tii + 1) * P], srck, [rPS[pb]], [rkdT[hh][tii] for hh in range(4)])
                    pend_tr = cur_tr if i < TPG else None
                sl = next_fill()
                for i in range(TPG):
                    ti = q * TPG + i
                    bank = pjb[0] % 4
                    pjb[0] += 1

                    def mm(e, sl=sl, i=i, bank=bank):
                        ins = None
                        for k in range(8):
                            ins = e.matmul(PSt[bank][:, :], lhsT=hT[:, k, i * P:(i + 1) * P], rhs=ring[sl][:, k, :], start=(k == 0), stop=(k == 7))
                        return ins
                    pg.op("pe", mm, r=rring[sl] + [rhT[i]], w=[rPS[bank]])
                    evac("act", Vd[:, ti, :], PSt[bank][:, :], [rPS[bank]], [rVd[ti]])
                if LVL < 5:
                    continue
                nkt = q * TPG + TPG
                steps = [(h, j) for h in range(4) for j in range(nkt)]
                info = {}
                deferred = []
                T = dtp
                zb = 3

                def post_stage(stage, h):
                    if stage == 0:
                        pg.op("act", lambda e: e.activation(out=T[0][:, :], in_=PSt[5][:, :], func=AF.Ln), r=[rPS[5]], w=[rdtp[0]])
                        pg.op("dve", lambda e: e.tensor_scalar(out=T[1][:, :], in0=PSt[4][:, :], scalar1=1.0, scalar2=None, op0=ALU.mult), r=[rPS[4]], w=[rdtp[1]])
                        pg.op("act", lambda e: e.activation(out=T[2][:, :], in_=PSt[7][:, :], func=AF.Ln), r=[rPS[7]], w=[rdtp[2]])
                        pg.op("dve", lambda e: e.tensor_scalar(out=T[3][:, :], in0=PSt[6][:, :], scalar1=1.0, scalar2=None, op0=ALU.mult), r=[rPS[6]], w=[rdtp[3]])
                    elif stage == 1:
                        pg.op("act", lambda e: e.activation(out=T[0][:, :], in_=T[0][:, :], func=AF.Exp, scale=-1.0), r=[rdtp[0]], w=[rdtp[0]])
                        pg.op("act", lambda e: e.activation(out=T[2][:, :], in_=T[2][:, :], func=AF.Exp, scale=-1.0), r=[rdtp[2]], w=[rdtp[2]])
                        pg.op("dve", lambda e: e.tensor_tensor(out=T[1][:, :], in0=T[1][:, :], in1=T[0][:, :], op=ALU.mult), r=[rdtp[0], rdtp[1]], w=[rdtp[1]])
                        pg.op("dve", lambda e: e.tensor_tensor(out=T[3][:, :], in0=T[3][:, :], in1=T[2][:, :], op=ALU.mult), r=[rdtp[2], rdtp[3]], w=[rdtp[3]])
                        pg.op("dve", lambda e: e.scalar_tensor_tensor(out=T[1][:, :], in0=T[3][:, :], scalar=neglam, in1=T[1][:, :], op0=ALU.mult, op1=ALU.add), r=[rdtp[1], rdtp[3], rlam], w=[rdtp[1]])
                    elif stage == 2:
                        pg.op("act", lambda e: e.activation(out=sqb[:, :], in_=T[1][:, :], func=AF.Square), r=[rdtp[1]], w=[rsqb])
                    else:
                        pg.op("pe", lambda e: e.matmul(PSt[zb][:, :], lhsT=onesb[:, :], rhs=sqb[:, :], start=True, stop=True), r=[rsqb, rconst], w=[rPS[zb]])
                        pg.op("act", lambda e: e.activation(out=T[0][:, :], in_=PSt[zb][:, :], func=AF.Ln, scale=1.0 / 128.0, bias=epsb[:, 0:1]), r=[rPS[zb], repsb], w=[rdtp[0]])
                        pg.op("act", lambda e: e.activation(out=T[0][:, :], in_=T[0][:, :], func=AF.Exp, scale=-0.5), r=[rdtp[0]], w=[rdtp[0]])
                        pg.op("dve", lambda e, h=h: e.scalar_tensor_tensor(out=oTd[:, h, :], in0=T[1][:, :], scalar=gsub, in1=T[0][:, :], op0=ALU.mult, op1=ALU.mult), r=[rdtp[0], rdtp[1], rlam], w=[roTd[h]])

                for si in range(len(steps) + 1):
                    if si < len(steps):
                        h, j = steps[si]
                        n0 = max(j - q * TPG, 0)
                        diag = j >= q * TPG
                        s1 = (dsb[0] % 2) * 2
                        dsb[0] += 1
                        s2 = s1 + 1
                        p1 = (dpb[0] % 2) * 2
                        p2 = p1 + 1
                        dpb[0] += 1
                        qs = slice(n0 * P, G)

                        def smm(e, h=h, j=j, n0=n0, diag=diag, s1=s1, s2=s2, qs=qs):
                            ins = None
                            for mp, sbank in ((0, s1), (1, s2)):
                                rows = slice(mp * 64, (mp + 1) * 64)
                                if not diag:
                                    ins = e.matmul(PSt[sbank][:, qs], lhsT=kdT[rows, h, j * P:(j + 1) * P], rhs=qdT[rows, h, qs], start=True, stop=True)
                                    continue
                                dq_ = slice(n0 * P, (n0 + 1) * P)
                                e.matmul(PSt[sbank][:, dq_], lhsT=kdT[rows, h, j * P:(j + 1) * P], rhs=qdT[rows, h, dq_], start=True, stop=False)
                                ins = e.matmul(PSt[sbank][:, dq_], lhsT=ident[:, :], rhs=maskb[:, :], start=False, stop=True)
                                if (n0 + 1) * P < G:
                                    rq_ = slice((n0 + 1) * P, G)
                                    ins = e.matmul(PSt[sbank][:, rq_], lhsT=kdT[rows, h, j * P:(j + 1) * P], rhs=qdT[rows, h, rq_], start=True, stop=True, skip_group_check=True)
                            return ins
                        pg.op("pe", smm, r=[rkdT[h][j], rconst] + rqdT[h], w=[rPS[s1], rPS[s2]])
                        pg.op("act", lambda e, s1=s1, p1=p1, qs=qs: e.activation(out=Pball[:, p1:p1 + 2, qs], in_=PSall[:, s1:s1 + 2, qs], func=AF.Exp, scale=0.125),
                              r=[rPS[s1], rPS[s2]], w=[rPb[p1], rPb[p2]])
                        info[si] = (p1, p2, qs)
                    sj = si - 1
                    if sj >= 0:
                        hh, jj = steps[sj]
                        q1, q2, qss = info.pop(sj)

                        def pv(e, jj=jj, q1=q1, q2=q2, qss=qss, hh=hh):
                            st, sp_ = (jj == 0), (jj == nkt - 1)
                            e.matmul(PSt[4][:, qss], lhsT=Vd[:, jj, hh * P:(hh + 1) * P], rhs=Pb[q1][:, qss], start=st, stop=sp_)
                            e.matmul(PSt[5][:, qss], lhsT=onesb[:, :], rhs=Pb[q1][:, qss], start=st, stop=sp_)
                            e.matmul(PSt[6][:, qss], lhsT=Vd[:, jj, hh * P:(hh + 1) * P], rhs=Pb[q2][:, qss], start=st, stop=sp_)
                            return e.matmul(PSt[7][:, qss], lhsT=onesb[:, :], rhs=Pb[q2][:, qss], start=st, stop=sp_)
                        pg.op("pe", pv, r=[rVd[jj], rconst, rPb[q1], rPb[q2]], w=[rPS[4], rPS[5], rPS[6], rPS[7]])
                        if jj == nkt - 1:
                            post_stage(0, hh)
                            for st_ in (1, 2, 3):
                                deferred.append((si + st_, st_, hh))
                    while deferred and deferred[0][0] <= si:
                        _, st_, hh_ = deferred.pop(0)
                        post_stage(st_, hh_)
                for _, st_, hh_ in deferred:
                    post_stage(st_, hh_)
                if LVL < 6:
                    continue
                BPF = int(os.environ.get("BPF", "1"))
                if q + 1 < NG and BPF:
                    norm_pre(q + 1)
                if q == NG - 1:
                    xphase_pre()
                sl0 = next_fill()
                sl1 = next_fill(NR - 2)
                for i in range(TPG):
                    ti = q * TPG + i
                    b0 = (pjb[0] % 2) * 2
                    pjb[0] += 1

                    def wo(e, i=i, b0=b0, sl0=sl0, sl1=sl1, q=q):
                        ins = None
                        for k in range(8):
                            lt = oTf[:, k, q * G + i * P:q * G + (i + 1) * P] if k < 4 else oTd[:, k - 4, i * P:(i + 1) * P]
                            for nh, sl in ((0, sl0), (1, sl1)):
                                ins = e.matmul(PSt[b0 + nh][:, :], lhsT=lt, rhs=ring[sl][:, k, :], start=(k == 0), stop=(k == 7))
                        return ins
                    pg.op("pe", wo, r=rring[sl0] + rring[sl1] + [roTf[q]] + roTd, w=[rPS[b0], rPS[b0 + 1]])
                    post_update(ti, (b0, b0 + 1), 1.0)
                    if q + 1 < NG and ((BPF == 2 and i == 1) or (BPF == 1 and i == TPG - 1)):
                        norm_post(q + 1)
                        ndone.add(("B", seq, q + 1))
                    if q == NG - 1 and i == TPG - 1:
                        xphase_post((6, 7))

        for seq in range(NSEQ):
            def load_x(t0, t1, seq=seq):
                for ti in range(t0, t1):
                    if (seq, ti) in preloaded:
                        continue
                    row = seq * S + ti * P
                    dma("sp", X[:, ti, :], x_d[row:row + P, :], f"xs{ti}", w=[rX[ti]])
            load_x(0, TPG)
            late_x.append(lambda: load_x(TPG, NT))
            last_stage = stages[-1]
            if "ffn1" in stages:
                ffn_phase(seq, 0, last_stage == "ffn1")
            if "mix" in stages:
                mix_a(seq)
                mix_b(seq)
            if "ffn2" in stages:
                ffn_phase(seq, 1, last_stage == "ffn2")
            if last_stage == "mix":
                for ti in range(NT):
                    row = seq * S + ti * P
                    dma("sp", out_d[row:row + P, :], X[:, ti, :], f"xs{ti}", r=[rX[ti]])
        pg.finalize()

        sems = {}
        for n in sorted(sem_names) + ENGS:
            sems[n] = es.enter_context(nc.semaphore("s_" + n))
        print("n_sems", len(sems), {e: len(pg.q[e]) for e in ENGS})
        block = es.enter_context(nc.Block())
        store_events = [(o.ev) for o in pg.q["sp"] if o.dma is not None and o.dma.startswith("xs")]
        final = {}
        for (s_, v) in store_events:
            final[s_] = max(final.get(s_, 0), v)

        for eng in ENGS:
            def body(e, eng=eng):
                known = {}
                for o in pg.q[eng]:
                    need = {}
                    for d in o.deps:
                        s_, v = d.ev
                        if v > need.get(s_, 0):
                            need[s_] = v
                    for s_, v in need.items():
                        if known.get(s_, 0) >= v:
                            continue
                        e.wait_ge(sems[s_], v)
                        known[s_] = v
                    ins = o.fn(e)
                    if o.dma is not None:
                        ins.then_inc(sems[o.ev[0]], 16)
                    elif o.need:
                        ins.then_inc(sems[eng], 1)
                if eng == "sp":
                    for s_, v in final.items():
                        e.wait_ge(sems[s_], v)
            getattr(block, BLK[eng])(body)
    return nc


_CACHE = {}


def _consts():
    half = 8
    inv_freq = (500000.0 ** (-np.arange(0, 16, 2, dtype=np.float32) / np.float32(16))).astype(np.float32)
    ang = np.arange(S, dtype=np.float32)[:, None] * inv_freq[None, :]
    cs = np.stack([np.tile(np.cos(ang), (1, 8)), np.tile(np.sin(ang), (1, 8))]).astype(np.float32)
    ident = np.eye(P, dtype=np.float32)
    sidx = np.arange(P)[:, None]
    tidx = np.arange(P)[None, :]
    mask = np.where(sidx > tidx, np.float32(NEG), np.float32(0.0)).astype(np.float32)
    return cs, np.stack([ident, mask]).astype(np.float32)


def make_in_maps(inputs, n_cores=8):
    f = lambda a: np.ascontiguousarray(np.asarray(a, dtype=np.float32))
    x = f(inputs["x"])
    cs, cm = _consts()
    gains = np.concatenate([f(inputs[k]).reshape(1, D) for k in ("ffn1_pre_g", "ffn1_post_g", "mix_pre_g", "mix_post_g", "ffn2_pre_g", "ffn2_post_g")], axis=0)
    lamv = np.concatenate([f(inputs[k]).reshape(1, 64) for k in ("diff_lambda_q1", "diff_lambda_k1", "diff_lambda_q2", "diff_lambda_k2")], axis=0)
    shared = {
        "wg1": f(inputs["ffn1_w_gate"])[0], "wu1": f(inputs["ffn1_w_up"])[0], "wd1": f(inputs["ffn1_w_down"])[0],
        "wg2": f(inputs["ffn2_w_gate"])[0], "wu2": f(inputs["ffn2_w_up"])[0], "wd2": f(inputs["ffn2_w_down"])[0],
        "w_in": f(inputs["w_in"])[0], "w_out": f(inputs["w_out"])[0],
        "gains": gains, "fb": f(inputs["fox_forget_b"]).reshape(1, 8), "lamv": lamv,
        "subg": f(inputs["diff_subln_g"]).reshape(1, 128), "cossin": cs, "cmats": cm,
    }
    maps = []
    for c in range(n_cores):
        m = dict(shared)
        m["x"] = np.ascontiguousarray(x[c * NSEQ:(c + 1) * NSEQ].reshape(NSEQ * S, D))
        maps.append(m)
    return maps


def kernel(**inputs):
    if "nc" not in _CACHE:
        _CACHE["nc"] = build_program()
    nc = _CACHE["nc"]
    maps = make_in_maps(inputs, 8)
    res = run_bass_kernel_spmd(nc, maps, core_ids=list(range(8)))
    outs = [np.asarray(r["out"]).reshape(NSEQ, S, D) for r in res.results]
    return np.concatenate(outs, axis=0).astype(np.float32)
```

```python
import math
import os
import numpy as np
import concourse.bass as bass
import concourse.mybir as mybir
from concourse.bass_utils import run_bass_kernel_spmd
from contextlib import ExitStack

F32 = mybir.dt.float32
BF16 = mybir.dt.bfloat16
AF = mybir.ActivationFunctionType
ALU = mybir.AluOpType
AX = mybir.AxisListType

P = 128
D = 1024
DFF = 2816
NCH = DFF // P
S = 2048
NT = S // P
G = 512
NG = S // G
TPG = G // P
NSEQ = 2
INW = 3080
EPS = 1e-6
LAM_INIT = 0.8 - 0.6 * math.exp(-0.3 * 0)
NEG = -30000.0
NR = 3
LVL = int(os.environ.get('MIXLVL', '9'))

ENGS = ["pe", "act", "dve", "pool", "sp"]
BLK = {"pe": "tensor", "act": "scalar", "dve": "vector", "pool": "gpsimd", "sp": "sync"}


class Res:
    __slots__ = ("name", "w", "rd")

    def __init__(self, name):
        self.name = name
        self.w = None
        self.rd = []


class Op:
    __slots__ = ("eng", "fn", "deps", "dma", "need", "ev")


class Prog:
    def __init__(self):
        self.q = {e: [] for e in ENGS}
        self.dma_cnt = {}

    def op(self, eng, fn, r=(), w=(), dma=None):
        o = Op()
        o.eng, o.fn, o.dma, o.need, o.ev = eng, fn, dma, False, None
        deps = []
        for x in r:
            if x.w is not None:
                deps.append(x.w)
        for x in w:
            if x.w is not None:
                deps.append(x.w)
            deps.extend(x.rd)
        for x in r:
            x.rd.append(o)
        for x in w:
            x.w = o
            x.rd = []
        seen = set()
        dd = []
        for d in deps:
            if d is o or id(d) in seen:
                continue
            seen.add(id(d))
            if d.eng == "pe" and eng == "pe" and d.dma is None:
                continue
            dd.append(d)
            d.need = True
        o.deps = dd
        if dma is not None:
            self.dma_cnt[dma] = self.dma_cnt.get(dma, 0) + 1
            o.ev = (dma, 16 * self.dma_cnt[dma])
        self.q[eng].append(o)
        return o

    def fence(self, old, new):
        pend = []
        for x in old:
            if x.w is not None:
                pend.append(x.w)
            pend.extend(x.rd)
        for x in new:
            x.w = None
            x.rd = list(pend)

    def finalize(self):
        for e in ENGS:
            c = 0
            for o in self.q[e]:
                if o.dma is None and o.need:
                    c += 1
                    o.ev = (e, c)


def build_program(stages=("ffn1", "mix", "ffn2"), dbg=False):
    nc = bass.Bass("TRN2", target_bir_lowering=False)
    dt = lambda name, shape, d=F32, kind="ExternalInput": nc.dram_tensor(name, shape, d, kind=kind).ap()
    x_d = dt("x", [NSEQ * S, D])
    out_d = dt("out", [NSEQ * S, D], kind="ExternalOutput")
    wg_d = [dt("wg1", [D, DFF]), dt("wg2", [D, DFF])]
    wu_d = [dt("wu1", [D, DFF]), dt("wu2", [D, DFF])]
    wd_d = [dt("wd1", [DFF, D]), dt("wd2", [DFF, D])]
    win_d = dt("w_in", [D, INW])
    wout_d = dt("w_out", [D, D])
    gains_d = dt("gains", [6, D])
    fb_d = dt("fb", [1, 8])
    lam_d = dt("lamv", [4, 64])
    sub_d = dt("subg", [1, 128])
    cs_d = dt("cossin", [2, S, 64])
    cm_d = dt("cmats", [2, P, P])

    es = ExitStack()
    with es:
        tot = nc.sbuf_bytes_remaining // P if False else 208000
        big = es.enter_context(nc.sbuf_tensor("big", [P, tot // 4], F32))
        B0 = nc.lookup_mloc(big).addr
        cur = [0]
        hi = [0]

        def alloc(name, shape, d, at=None):
            nbytes = int(np.prod(shape[1:])) * (4 if d == F32 else 2)
            nbytes = (nbytes + 31) // 32 * 32
            off = cur[0] if at is None else at
            t = nc.alloc_sbuf_tensor_at(name, list(shape), d, offset=B0 + off)
            if at is None:
                cur[0] += nbytes
            hi[0] = max(hi[0], off + nbytes)
            assert hi[0] <= tot, (name, hi[0], tot)
            return t

        X = alloc("X", [P, NT, D], F32)
        gpre = alloc("gpre", [P, D], F32)
        gpost = alloc("gpost", [P, D], F32)
        ring = [alloc(f"ring{i}", [P, 8, 512], BF16) for i in range(NR)]
        htok = [alloc(f"htok{i}", [P, D], BF16) for i in range(2)]
        junk = alloc("junk", [P, D], BF16)
        junk2 = alloc("junk2", [P, D], BF16)
        hT = alloc("hT", [P, 8, G], BF16)
        NPB = 4
        Pball = alloc("Pball", [P, NPB, G], BF16)
        Pb = [Pball[:, i, :] for i in range(NPB)]
        stat = alloc("stat", [P, 64], F32)
        ident = alloc("ident", [P, P], BF16)
        maskb = alloc("maskb", [P, P], BF16)
        onesb = alloc("onesb", [P, P], BF16)
        lamt = alloc("lamt", [P, 4, 64], F32)
        lsm = alloc("lsm", [P, 16], F32)
        fbt = alloc("fbt", [P, 4], F32)
        wgate = alloc("wgate", [P, 8, 8], BF16)
        rtmp = [alloc(f"rtmp{i}", [P, G], F32) for i in range(2)]
        epsb = alloc("epsb", [P, 2], F32)
        ovl = cur[0]
        aT = alloc("aT", [P, NCH, G], BF16)
        Wd = alloc("Wd", [P, NCH, D], BF16)
        silu = [alloc(f"silu{i}", [P, G], BF16) for i in range(2)]
        ffn_end = cur[0]
        cur[0] = ovl
        oTf = alloc("oTf", [P, 4, S], BF16)
        ab = cur[0]
        kx = [alloc(f"kx{h}", [P, S], BF16) for h in range(8)]
        qx = [alloc(f"qx{h}", [P, G], BF16) for h in range(8)]
        Vf = alloc("Vf", [P, NT, 4, 192], BF16)
        kx0 = ab
        cf = [alloc("cf0", [P, G], F32, at=kx0), alloc("cf1", [P, G], F32, at=kx0 + 2048)]
        cst = alloc("cst", [P, 6, G], BF16, at=kx0 + 4096)
        a_end = cur[0]
        cur[0] = ab
        kdT = alloc("kdT", [P, 4, S], BF16)
        qdT = alloc("qdT", [P, 4, G], BF16)
        Vd = alloc("Vd", [P, NT, 512], BF16)
        oTd = alloc("oTd", [P, 4, G], BF16)
        qkb = [alloc(f"qkb{i}", [P, 1024], BF16) for i in range(2)]
        rp = [alloc(f"rp{i}", [P, 8, 8], F32) for i in range(4)]
        dtp_off = cur[0]
        dtp = [alloc(f"dtp{i}", [P, G], F32) for i in range(4)]
        sqb = alloc("sqb", [P, G], BF16)
        qkf = alloc("qkf", [P, 1024], F32, at=dtp_off)
        cosT = alloc("cosT", [P, NT, 64], F32)
        sinT = alloc("sinT", [P, NT, 64], F32)
        b_end = cur[0]
        print("SBUF bytes/partition: common", ovl, "ffn", ffn_end, "mixA", a_end, "mixB", b_end, "limit", tot)

        PSall = es.enter_context(nc.psum_tensor("psall", [P, 8, 512], F32))
        PSallb = PSall.bitcast(BF16)
        PSt = [PSall[:, i, :] for i in range(8)]
        PSb = [PSallb[:, i, :] for i in range(8)]

        R = Res
        rX = [R(f"X{i}") for i in range(NT)]
        rgpre, rgpost = R("gpre"), R("gpost")
        rring = [[R(f"ring{i}a"), R(f"ring{i}b")] for i in range(NR)]
        rhtok = [R("htok0"), R("htok1")]
        rjunk = R("junk")
        rhT = [R(f"hT{i}") for i in range(TPG)]
        rPb = [R(f"Pb{i}") for i in range(NPB)]
        rstat = {}
        rconst = R("const")
        rlam = R("lam")
        rwgate = R("wgate")
        rrtmp = [R("rtmp0"), R("rtmp1")]
        rPS = [R(f"ps{i}") for i in range(8)]
        raT = [R(f"aT{c}") for c in range(NCH)]
        rWd = R("Wd")
        rWdp = [R(f"Wd{c}") for c in range(NCH // 2)]
        rsilu = [R("silu0"), R("silu1")]
        FFN_RES = raT + [rWd] + rWdp + rsilu
        roTf = [R(f"oTf{q}") for q in range(NG)]
        rkx = [[R(f"kx{h}_{q}") for q in range(NG)] for h in range(8)]
        rkxa = [[[R(f"kxa{h}_{q}_{i}") for i in range(3)] for q in range(NG)] for h in range(8)]
        rqx = [R(f"qx{h}") for h in range(8)]
        rqxa = [[R(f"qxa{h}_{i}") for i in range(3)] for h in range(8)]
        rkqinit = R("kqinit")
        rVf = [R(f"Vf{t}") for t in range(NT)]
        rVfones = R("Vfones")
        rcst = R("cst")
        rcf = [R("cf0"), R("cf1")]
        rcarry = R("carry")
        A_RES = [x for l in rkx for x in l] + [x for l in rkxa for m_ in l for x in m_] + rqx + [x for l in rqxa for x in l] + [rkqinit] + rVf + [rVfones, rcst] + rcf
        rkdT = [[R(f"kdT{h}_{t}") for t in range(NT)] for h in range(4)]
        rqdT = [[R(f"qdT{h}_{i}") for i in range(TPG)] for h in range(4)]
        rVd = [R(f"Vd{t}") for t in range(NT)]
        roTd = [R(f"oTd{h}") for h in range(4)]
        rqkb = [R("qkb0"), R("qkb1")]
        rrp = [R(f"rp{i}") for i in range(4)]
        rdtp = [R(f"dtp{i}") for i in range(4)]
        rsqb = R("sqb")
        rcsx = R("csx")
        B_RES = [x for l in rkdT for x in l] + [x for l in rqdT for x in l] + rVd + roTd + rqkb + rrp + rdtp + [rsqb, rcsx]
        MIX_RES = roTf

        pg = Prog()
        sem_names = set()
        rcarry0 = R("carry0")

        def dma(eng, out, in_, sem, r=(), w=()):
            sem_names.add(sem)
            return pg.op(eng, lambda e: e.dma_start(out=out, in_=in_), r=r, w=w, dma=sem)

        fills = []
        fill_pos = [0]
        issued = [0]

        def plan_fills():
            def ffn_f(which):
                gv = wg_d[which].rearrange("(k p) n -> p k n", p=P)
                uv = wu_d[which].rearrange("(k p) n -> p k n", p=P)
                for q in range(NG):
                    for cp in range(NCH // 2):
                        fills.append([(0, 256, gv[:, :, cp * 256:(cp + 1) * 256], 0), (256, 512, uv[:, :, cp * 256:(cp + 1) * 256], 1)])
            wv = win_d.rearrange("(k p) n -> p k n", p=P)
            ov = wout_d.rearrange("(k p) n -> p k n", p=P)
            for s in range(NSEQ):
                if "ffn1" in stages:
                    ffn_f(0)
                if "mix" in stages:
                    for q in range(NG):
                        for c0 in (0, 512, 1024):
                            fills.append([(0, 512, wv[:, :, c0:c0 + 512], 0)])
                    for q in range(NG):
                        for c0 in (1544, 2056, 2568):
                            fills.append([(0, 512, wv[:, :, c0:c0 + 512], 0)])
                        for c0 in (0, 512):
                            fills.append([(0, 512, ov[:, :, c0:c0 + 512], 0)])
                if "ffn2" in stages:
                    ffn_f(1)

        def ensure(n):
            while issued[0] <= n and issued[0] < len(fills):
                i = issued[0]
                sl = i % NR
                for (c0, c1, src, part) in fills[i]:
                    ws = [rring[sl][part]] if len(fills[i]) == 2 else rring[sl]
                    dma("pool", ring[sl][:, :, c0:c1], src, f"ring{sl}_{part}", w=ws)
                issued[0] += 1

        def next_fill(look=NR - 1):
            n = fill_pos[0]
            fill_pos[0] += 1
            ensure(n + look)
            return n % NR

        plan_fills()

        stg = rtmp[0]
        dma("sp", stg[:, 0:128], cm_d[0], "c0", w=[rrtmp[0]])
        dma("sp", stg[:, 128:256], cm_d[1], "c1", w=[rrtmp[0]])
        for i in range(4):
            dma("sp", lamt[:, i, :], lam_d[i:i + 1, :].to_broadcast([P, 64]), f"c4_{i}", w=[rlam])
        dma("sp", lsm[:, 8:9], sub_d.rearrange("o d -> d o"), "c5", w=[rlam])
        dma("sp", fbt[96:104, 0:1], fb_d.rearrange("o h -> h o"), "c6", w=[rlam])
        pg.op("dve", lambda e: e.memset(fbt[96:104, 2:3], 1.0), w=[rcarry0])
        pg.op("dve", lambda e: e.tensor_scalar(out=fbt[96:104, 3:4], in0=fbt[96:104, 0:1], scalar1=-1.0, scalar2=None, op0=ALU.mult), r=[rlam], w=[rlam])
        pg.op("dve", lambda e: e.tensor_copy(out=ident[:, :], in_=stg[:, 0:128]), r=[rrtmp[0]], w=[rconst])
        pg.op("dve", lambda e: e.tensor_copy(out=maskb[:, :], in_=stg[:, 128:256]), r=[rrtmp[0]], w=[rconst])
        pg.op("dve", lambda e: e.memset(onesb[:, :], 1.0), w=[rconst])
        pg.op("dve", lambda e: e.tensor_tensor(out=lamt[:, 0, :], in0=lamt[:, 0, :], in1=lamt[:, 1, :], op=ALU.mult), r=[rlam], w=[rlam])
        pg.op("dve", lambda e: e.tensor_tensor(out=lamt[:, 2, :], in0=lamt[:, 2, :], in1=lamt[:, 3, :], op=ALU.mult), r=[rlam], w=[rlam])
        pg.op("dve", lambda e: e.reduce_sum(out=lsm[:, 0:1], in_=lamt[:, 0, :], axis=AX.X), r=[rlam], w=[rlam])
        pg.op("dve", lambda e: e.reduce_sum(out=lsm[:, 1:2], in_=lamt[:, 2, :], axis=AX.X), r=[rlam], w=[rlam])
        pg.op("act", lambda e: e.activation(out=lsm[:, 2:4], in_=lsm[:, 0:2], func=AF.Exp), r=[rlam], w=[rlam])
        pg.op("dve", lambda e: e.scalar_tensor_tensor(out=lsm[:, 4:5], in0=lsm[:, 3:4], scalar=-LAM_INIT, in1=lsm[:, 2:3], op0=ALU.add, op1=ALU.subtract), r=[rlam], w=[rlam])
        pg.op("dve", lambda e: e.tensor_scalar(out=lsm[:, 9:10], in0=lsm[:, 8:9], scalar1=1.0 - LAM_INIT, scalar2=None, op0=ALU.mult), r=[rlam], w=[rlam])
        neglam = lsm[:, 4:5]
        gsub = lsm[:, 9:10]

        stat_i = [0]

        def stat_cols(n):
            c = stat_i[0]
            if c + n > 64:
                c = 0
            stat_i[0] = c + n
            key = (c, n)
            if key not in rstat:
                rstat[key] = R(f"stat{c}_{n}")
            return c, rstat[key]

        rstatN = [R("statN0"), R("statN1")]
        rstatP = [R("statP0"), R("statP1")]
        ngc = [0]
        puc = [0]

        gpre_cur = [None]

        def load_gpre(ipre):
            if gpre_cur[0] != ipre:
                dma("sp", gpre[:, :], gains_d[ipre:ipre + 1, :].to_broadcast([P, D]), "gpre", w=[rgpre])
                gpre_cur[0] = ipre

        def load_gains(ipre, ipost):
            load_gpre(ipre)
            dma("sp", gpost[:, :], gains_d[ipost:ipost + 1, :].to_broadcast([P, D]), "gpost", w=[rgpost])

        phases = []
        if tuple(stages) == ("ffn1", "mix", "ffn2"):
            for sq in range(NSEQ):
                phases += [(("ffn", sq, 0, 0), 0), (("A", sq, 0), 2), (("B", sq, 0), 2), (("ffn", sq, 1, 0), 4)]
        phase_i = [-1]

        def xphase_pre():
            i = phase_i[0] + 1
            if not phases or i >= len(phases):
                return
            load_gpre(phases[i][1])
            norm_pre(0)

        def xphase_post(tbanks):
            i = phase_i[0] + 1
            if not phases or i >= len(phases):
                return
            norm_post(0, tbanks)
            ndone.add(phases[i][0])

        tb = [0]

        npre = {}
        ndone = set()

        def norm_pre(q):
            par = ngc[0] % 2
            ngc[0] += 1
            c0 = 32 * par
            rstat_all = rstatN[par]
            npre[q] = (c0, rstat_all)
            for i in range(TPG):
                ti = q * TPG + i
                pg.op("dve", lambda e, ti=ti, i=i: e.scalar_tensor_tensor(out=junk2[:, :], in0=X[:, ti, :], scalar=1.0, in1=X[:, ti, :], op0=ALU.mult, op1=ALU.mult, accum_out=stat[:, c0 + i:c0 + i + 1]),
                      r=[rX[ti]], w=[rstat_all])
            pg.op("act", lambda e: e.activation(out=stat[:, c0 + 4:c0 + 8], in_=stat[:, c0:c0 + 4], func=AF.Ln, scale=1.0 / D, bias=epsb[:, 0:1]), r=[rstat_all], w=[rstat_all])
            pg.op("act", lambda e: e.activation(out=stat[:, c0 + 8:c0 + 12], in_=stat[:, c0 + 4:c0 + 8], func=AF.Exp, scale=-0.5), r=[rstat_all], w=[rstat_all])

        def norm_post(q, tbanks=(6, 7)):
            c0, rstat_all = npre.pop(q)
            for i in range(TPG):
                ti = q * TPG + i
                b = tb[0] % 2
                tb[0] += 1
                pg.op("dve", lambda e, ti=ti, i=i, b=b: e.scalar_tensor_tensor(out=htok[b][:, :], in0=X[:, ti, :], scalar=stat[:, c0 + 8 + i:c0 + 9 + i], in1=gpre[:, :], op0=ALU.mult, op1=ALU.mult),
                      r=[rX[ti], rstat_all, rgpre], w=[rhtok[b]])
                pb = tbanks[b]

                def tr(e, b=b, pb=pb):
                    ins = None
                    for k in range(8):
                        ins = e.transpose(out=PSb[pb][:, k * P:(k + 1) * P], in_=htok[b][:, k * P:(k + 1) * P], identity=ident[:, :])
                    return ins
                pg.op("pe", tr, r=[rhtok[b], rconst], w=[rPS[pb]])
                pg.op("act", lambda e, i=i, pb=pb: e.activation(out=hT[:, :, i * P:(i + 1) * P], in_=PSb[pb][:, :].rearrange("p (k t) -> p k t", k=8), func=AF.Copy),
                      r=[rPS[pb]], w=[rhT[i]])

        def norm_group(q, key=None, tbanks=(6, 7)):
            if key is not None and key in ndone:
                return
            norm_pre(q)
            norm_post(q, tbanks)
            while late_x:
                late_x.pop(0)()

        def post_update(ti, banks, half_scale, store_seq=None):
            par = puc[0] % 2
            puc[0] += 1
            c0 = 16 + 32 * par
            rstat_all = rstatP[par]
            for nh in range(2):
                pg.op("act", lambda e, nh=nh: e.activation(out=junk[:, nh * 512:(nh + 1) * 512], in_=PSt[banks[nh]][:, :], func=AF.Square, accum_out=stat[:, c0 + nh:c0 + nh + 1]),
                      r=[rPS[banks[nh]]], w=[rstat_all])
            pg.op("dve", lambda e: e.tensor_tensor(out=stat[:, c0 + 2:c0 + 3], in0=stat[:, c0:c0 + 1], in1=stat[:, c0 + 1:c0 + 2], op=ALU.add), r=[rstat_all], w=[rstat_all])
            sc = (1.0 / D) / (half_scale * half_scale)
            bt = epsb[:, 1:2] if half_scale == 0.5 else epsb[:, 0:1]
            pg.op("act", lambda e: e.activation(out=stat[:, c0 + 3:c0 + 4], in_=stat[:, c0 + 2:c0 + 3], func=AF.Ln, scale=sc, bias=bt), r=[rstat_all], w=[rstat_all])
            pg.op("act", lambda e: e.activation(out=stat[:, c0 + 4:c0 + 5], in_=stat[:, c0 + 3:c0 + 4], func=AF.Exp, scale=-0.5), r=[rstat_all], w=[rstat_all])
            for nh in range(2):
                pg.op("dve", lambda e, nh=nh: e.scalar_tensor_tensor(out=rtmp[nh][:, :], in0=PSt[banks[nh]][:, :], scalar=stat[:, c0 + 4:c0 + 5], in1=gpost[:, nh * 512:(nh + 1) * 512], op0=ALU.mult, op1=ALU.mult),
                      r=[rPS[banks[nh]], rstat_all, rgpost], w=[rrtmp[nh]])
                pg.op("dve", lambda e, nh=nh: e.tensor_tensor(out=X[:, ti, nh * 512:(nh + 1) * 512], in0=X[:, ti, nh * 512:(nh + 1) * 512], in1=rtmp[nh][:, :], op=ALU.add),
                      r=[rrtmp[nh], rX[ti]], w=[rX[ti]])
            if store_seq is not None:
                row = store_seq * S + ti * P
                dma("sp", out_d[row:row + P, :], X[:, ti, :], f"xs{ti}", r=[rX[ti]])
                if store_seq + 1 < NSEQ:
                    row = (store_seq + 1) * S + ti * P
                    dma("sp", X[:, ti, :], x_d[row:row + P, :], f"xs{ti}", w=[rX[ti]])
                    preloaded.add((store_seq + 1, ti))

        repsb = R("epsb")
        pg.op("dve", lambda e: e.memset(epsb[:, 0:1], EPS), w=[repsb])
        pg.op("dve", lambda e: e.memset(epsb[:, 1:2], 4.0 * EPS), w=[repsb])
        rconst_eps = repsb

        gub = [0]
        dbk = [0]
        preloaded = set()
        late_x = []
        ovl_user = [None]
        OVL = {"ffn": FFN_RES, "A": A_RES + MIX_RES, "B": B_RES + MIX_RES}

        def enter_overlay(name):
            old = ovl_user[0]
            if old == name:
                return
            if old is not None:
                if old == "A" and name == "B":
                    pg.fence(A_RES, B_RES)
                else:
                    new = OVL[name]
                    pg.fence(OVL[old], new)
            ovl_user[0] = name

        def ffn_phase(seq, which, last):
            phase_i[0] += 1
            enter_overlay("ffn")
            load_gains(0 if which == 0 else 4, 1 if which == 0 else 5)
            wdv = wd_d[which].rearrange("(c p) n -> p c n", p=P)
            first = [True]
            for q in range(NG):
                norm_group(q, key=("ffn", seq, which, q), tbanks=(0, 1))
                for cp in range(NCH // 2):
                    sl = next_fill()
                    if q == 0:
                        dma("pool", Wd[:, 2 * cp:2 * cp + 2, :], wdv[:, 2 * cp:2 * cp + 2, :], f"wd{cp % 2}", w=[rWdp[cp]])
                    for cc in range(2):
                        c = cp * 2 + cc
                        gb = (gub[0] % 2) * 2
                        gub[0] += 1
                        ub = gb + 1

                        def mm(e, sl=sl, off=0, cc=cc, bank=gb):
                            ins = None
                            for k in range(8):
                                ins = e.matmul(PSt[bank][:, :], lhsT=ring[sl][:, k, off + cc * P: off + (cc + 1) * P], rhs=hT[:, k, :], start=(k == 0), stop=(k == 7))
                            return ins
                        pg.op("pe", lambda e, sl=sl, cc=cc, gb=gb: mm(e, sl, 0, cc, gb), r=[rring[sl][0]] + rhT, w=[rPS[gb]])
                        pg.op("pe", lambda e, sl=sl, cc=cc, ub=ub: mm(e, sl, 256, cc, ub), r=[rring[sl][1]] + rhT, w=[rPS[ub]])
                        sb = c % 2
                        pg.op("act", lambda e, gb=gb, sb=sb: e.activation(out=silu[sb][:, :], in_=PSt[gb][:, :], func=AF.Silu), r=[rPS[gb]], w=[rsilu[sb]])
                        pg.op("dve", lambda e, c=c, ub=ub, sb=sb: e.tensor_tensor(out=aT[:, c, :], in0=silu[sb][:, :], in1=PSt[ub][:, :], op=ALU.mult), r=[rsilu[sb], rPS[ub]], w=[raT[c]])
                for i in range(TPG):
                    ti = q * TPG + i
                    d0 = 4 + (dbk[0] % 2) * 2
                    dbk[0] += 1

                    def dn(e, i=i, d0=d0):
                        ins = None
                        for c in range(NCH):
                            for nh in range(2):
                                ins = e.matmul(PSt[d0 + nh][:, :], lhsT=aT[:, c, i * P:(i + 1) * P], rhs=Wd[:, c, nh * 512:(nh + 1) * 512], start=(c == 0), stop=(c == NCH - 1))
                        return ins
                    pg.op("pe", dn, r=raT + rWdp, w=[rPS[d0], rPS[d0 + 1]])
                    if q + 1 < NG and i == 0:
                        norm_pre(q + 1)
                    if q + 1 < NG and i == 1:
                        norm_post(q + 1, tbanks=(0, 1))
                        ndone.add(("ffn", seq, which, q + 1))
                    if q == NG - 1 and i == 0:
                        xphase_pre()
                    if q == NG - 1 and i == 1:
                        xphase_post((0, 1))
                    post_update(ti, (d0, d0 + 1), 0.5, store_seq=(seq if last else None))

        def evac(eng, out, in_, r, w, scale=None):
            if eng == "act":
                if scale is None:
                    return pg.op("act", lambda e: e.activation(out=out, in_=in_, func=AF.Copy), r=r, w=w)
                return pg.op("act", lambda e: e.activation(out=out, in_=in_, func=AF.Copy, scale=scale), r=r, w=w)
            if scale is None:
                return pg.op("dve", lambda e: e.tensor_copy(out=out, in_=in_), r=r, w=w)
            return pg.op("dve", lambda e: e.tensor_scalar(out=out, in0=in_, scalar1=scale, scalar2=None, op0=ALU.mult), r=r, w=w)

        pjb = [0]
        dsb = [0]
        dpb = [0]
        sbk = [0]
        pbk = [0]

        def mix_a(seq):
            phase_i[0] += 1
            enter_overlay("A")
            load_gains(2, 3)
            dma("pool", wgate[:, :, :], win_d.rearrange("(k p) n -> p k n", p=P)[:, :, 1536:1544], "wgate", w=[rwgate])
            for h in range(8):
                pg.op("pool", lambda e, h=h: e.memset(kx[h][64:70, :], 1.0), w=[rkqinit])
                pg.op("pool", lambda e, h=h: e.memset(qx[h][64:70, :], 1.0), w=[rkqinit])
            pg.op("pool", lambda e: e.memset(Vf[:, :, :, 64:128], 1.0), w=[rVfones])
            for q in range(NG):
                norm_group(q, key=("A", seq, q))
                tok = slice(q * G, (q + 1) * G)
                def gate_part1(qq, bank):
                    tk = slice(qq * G, (qq + 1) * G)

                    def mmg(e, bank=bank):
                        ins = None
                        for k in range(8):
                            ins = e.matmul(PSt[bank][0:8, :], lhsT=wgate[:, k, :], rhs=hT[:, k, :], start=(k == 0), stop=(k == 7))
                        return ins
                    pg.op("pe", mmg, r=[rwgate] + rhT, w=[rPS[bank]])
                    pg.op("act", lambda e, bank=bank: e.activation(out=cf[0][96:104, :], in_=PSt[bank][0:8, :], func=AF.Exp, scale=-1.0, bias=fbt[96:104, 3:4]), r=[rPS[bank], rlam], w=[rcf[0]])
                    pg.op("act", lambda e: e.activation(out=cf[0][96:104, :], in_=cf[0][96:104, :], func=AF.Ln, bias=fbt[96:104, 2:3]), r=[rcf[0], rcarry0], w=[rcf[0]])
                    ini = 0.0 if qq == 0 else fbt[96:104, 1:2]
                    rr = [rcf[0], rcarry0] + ([] if qq == 0 else [rcarry])
                    pg.op("dve", lambda e: e.tensor_tensor_scan(out=cf[1][96:104, :], data0=fbt[96:104, 2:3].to_broadcast([8, G]), data1=cf[0][96:104, :], initial=ini, op0=ALU.mult, op1=ALU.subtract), r=rr, w=[rcf[1]])
                    pg.op("dve", lambda e: e.tensor_copy(out=fbt[96:104, 1:2], in_=cf[1][96:104, G - 1:G]), r=[rcf[1]], w=[rcarry])
                    pg.op("pool", lambda e: e.tensor_copy(out=cst[96:104, 0, :], in_=cf[1][96:104, :]), r=[rcf[1]], w=[rcst])
                    pg.op("pool", lambda e: e.tensor_tensor(out=cf[0][96:104, :], in0=cf[1][96:104, :], in1=cst[96:104, 0, :], op=ALU.subtract), r=[rcf[1], rcst], w=[rcf[0]])
                    pg.op("pool", lambda e: e.tensor_copy(out=cst[96:104, 1, :], in_=cf[0][96:104, :]), r=[rcf[0]], w=[rcst])
                    pg.op("pool", lambda e: e.tensor_tensor(out=cf[0][96:104, :], in0=cf[0][96:104, :], in1=cst[96:104, 1, :], op=ALU.subtract), r=[rcf[0], rcst], w=[rcf[0]])
                    pg.op("pool", lambda e: e.tensor_copy(out=cst[96:104, 2, :], in_=cf[0][96:104, :]), r=[rcf[0]], w=[rcst])
                    pg.op("pool", lambda e: e.tensor_scalar(out=cst[96:104, 3:6, :], in0=cst[96:104, 0:3, :], scalar1=-1.0, scalar2=1.0, op0=ALU.mult, op1=ALU.mult), r=[rcst], w=[rcst])
                    for h in range(8):
                        for i3 in range(3):
                            dma("sp", kx[h][67 + i3:68 + i3, tk], cst[96 + h:97 + h, 3 + i3, :], f"ka{h}_{i3}", r=[rcst, rkqinit], w=[rkxa[h][qq][i3]])

                def gate_part2():
                    for h in range(8):
                        for i3 in range(3):
                            dma("sp", qx[h][64 + i3:65 + i3, :], cst[96 + h:97 + h, i3, :], f"qa{h}_{i3}", r=[rcst, rkqinit], w=[rqxa[h][i3]])

                if q == 0:
                    bank = pjb[0] % 2
                    pjb[0] += 1
                    gate_part1(0, bank)
                gate_part2()
                for which_qk in range(2):
                    sl = next_fill()
                    for m in range(4):
                        bank = pjb[0] % 2
                        pjb[0] += 1

                        def mm(e, sl=sl, m=m, bank=bank):
                            ins = None
                            for k in range(8):
                                ins = e.matmul(PSt[bank][:, :], lhsT=ring[sl][:, k, m * P:(m + 1) * P], rhs=hT[:, k, :], start=(k == 0), stop=(k == 7))
                            return ins
                        pg.op("pe", mm, r=rring[sl] + rhT, w=[rPS[bank]])
                        for par in range(2):
                            h = 2 * m + par
                            src = PSt[bank][par * 64:(par + 1) * 64, :]
                            extra = [] if par == 0 else ([rqx[h - 1]] if which_qk == 0 else [rkx[h - 1][q]])
                            if which_qk == 0:
                                evac("act" if par == 0 else "dve", qx[h][0:64, :], src, [rPS[bank]] + extra, [rqx[h]], scale=0.125)
                            else:
                                evac("act" if par == 0 else "dve", kx[h][0:64, tok], src, [rPS[bank]] + extra, [rkx[h][q]])

                sl = next_fill()
                for i in range(TPG):
                    ti = q * TPG + i
                    bank = 2 + pjb[0] % 2
                    pjb[0] += 1

                    def mm(e, sl=sl, i=i, bank=bank):
                        ins = None
                        for k in range(8):
                            ins = e.matmul(PSt[bank][:, :], lhsT=hT[:, k, i * P:(i + 1) * P], rhs=ring[sl][:, k, :], start=(k == 0), stop=(k == 7))
                        return ins
                    pg.op("pe", mm, r=rring[sl] + [rhT[i]], w=[rPS[bank]])
                    v4 = PSt[bank][:, :].rearrange("p (m e d) -> p m e d", m=4, e=2)
                    evac("act", Vf[:, ti, :, 0:64], v4[:, :, 0, :], [rPS[bank]], [rVf[ti]])
                    evac("dve", Vf[:, ti, :, 128:192], v4[:, :, 1, :], [rPS[bank], rVfones], [rVf[ti]])
                if LVL < 3:
                    continue
                nkt = q * TPG + TPG
                LA = 2
                steps = [(h, j) for h in range(8) for j in range(nkt)]
                info = {}
                for si in range(len(steps) + LA):
                    if si < len(steps):
                        h, j = steps[si]
                        n0 = max(j - q * TPG, 0)
                        diag = j >= q * TPG
                        sb = sbk[0] % 3
                        sbk[0] += 1
                        pb = pbk[0] % NPB
                        pbk[0] += 1
                        qs = slice(n0 * P, G)

                        def smm(e, h=h, j=j, n0=n0, diag=diag, sb=sb, qs=qs):
                            if not diag:
                                return e.matmul(PSt[sb][:, qs], lhsT=kx[h][0:70, j * P:(j + 1) * P], rhs=qx[h][0:70, qs], start=True, stop=True)
                            dq_ = slice(n0 * P, (n0 + 1) * P)
                            e.matmul(PSt[sb][:, dq_], lhsT=kx[h][0:70, j * P:(j + 1) * P], rhs=qx[h][0:70, dq_], start=True, stop=False)
                            ins = e.matmul(PSt[sb][:, dq_], lhsT=ident[:, :], rhs=maskb[:, :], start=False, stop=True)
                            if (n0 + 1) * P < G:
                                rq_ = slice((n0 + 1) * P, G)
                                ins = e.matmul(PSt[sb][:, rq_], lhsT=kx[h][0:70, j * P:(j + 1) * P], rhs=qx[h][0:70, rq_], start=True, stop=True, skip_group_check=True)
                            return ins
                        pg.op("pe", smm, r=[rkx[h][j // TPG], rqx[h], rconst] + rkxa[h][j // TPG] + rqxa[h], w=[rPS[sb]])
                        pg.op("act", lambda e, sb=sb, pb=pb, qs=qs: e.activation(out=Pb[pb][:, qs], in_=PSt[sb][:, qs], func=AF.Exp), r=[rPS[sb]], w=[rPb[pb]])
                        info[si] = (pb, qs)
                    if si == len(steps) // 2 and q + 1 < NG:
                        norm_pre(q + 1)
                    if si == len(steps) // 2 + 4 and q + 1 < NG:
                        norm_post(q + 1)
                        ndone.add(("A", seq, q + 1))
                    if si == len(steps) // 2 + 8 and q + 1 < NG:
                        gate_part1(q + 1, 3)
                    if si == len(steps) // 2 and q == NG - 1:
                        xphase_pre()
                    if si == len(steps) // 2 + 4 and q == NG - 1:
                        xphase_post((6, 7))
                    sj = si - LA
                    if sj >= 0:
                        h, jj = steps[sj]
                        pbb, qss = info.pop(sj)
                        ob = 4 + h % 2
                        m, par = h // 2, h % 2
                        vsl = slice(0, 128) if par == 0 else slice(64, 192)
                        pg.op("pe", lambda e, jj=jj, pbb=pbb, qss=qss, m=m, vsl=vsl, ob=ob: e.matmul(PSt[ob][:, qss], lhsT=Vf[:, jj, m, vsl], rhs=Pb[pbb][:, qss], start=(jj == 0), stop=(jj == nkt - 1)),
                              r=[rVf[jj], rVfones, rPb[pbb]], w=[rPS[ob]])
                        if jj == nkt - 1:
                            nrow = slice(0, 64) if par == 0 else slice(64, 128)
                            drow = slice(64, 128) if par == 0 else slice(0, 64)
                            rt = rtmp[h % 2]
                            if q <= 1:
                                pg.op("act", lambda e, ob=ob, drow=drow, rt=rt: e.activation(out=rt[drow, :], in_=PSt[ob][drow, :], func=AF.Ln), r=[rPS[ob]], w=[rrtmp[h % 2]])
                                pg.op("act", lambda e, drow=drow, rt=rt: e.activation(out=rt[drow, :], in_=rt[drow, :], func=AF.Exp, scale=-1.0), r=[rrtmp[h % 2]], w=[rrtmp[h % 2]])
                            else:
                                pg.op("dve", lambda e, ob=ob, drow=drow, rt=rt: e.reciprocal(out=rt[drow, :], in_=PSt[ob][drow, :]), r=[rPS[ob]], w=[rrtmp[h % 2]])
                            pg.op("dve", lambda e, ob=ob, nrow=nrow, drow=drow, rt=rt, m=m, tok=tok: e.tensor_tensor(out=oTf[nrow, m, tok], in0=PSt[ob][nrow, :], in1=rt[drow, :], op=ALU.mult),
                                  r=[rPS[ob], rrtmp[h % 2]], w=[roTf[q]])

        def mix_b(seq):
            phase_i[0] += 1
            enter_overlay("B")
            if LVL < 4:
                return
            dma("sp", cosT[:, :, :], cs_d[0].rearrange("(t p) f -> p t f", p=P), "c2", w=[rcsx])
            dma("sp", sinT[:, :, :], cs_d[1].rearrange("(t p) f -> p t f", p=P), "c3", w=[rcsx])
            for q in range(NG):
                norm_group(q, key=("B", seq, q))
                tok = slice(q * G, (q + 1) * G)
                sl_q = next_fill()
                sl_k = next_fill(NR - 2)
                pend_tr = None
                for i in range(TPG + 1):
                    if i < TPG:
                        ti = q * TPG + i
                        bq = (pjb[0] % 2) * 2
                        pjb[0] += 1
                        bk = bq + 1
                        b2 = i % 2

                        def mm(e, sl, i, bank):
                            ins = None
                            for k in range(8):
                                ins = e.matmul(PSt[bank][:, :], lhsT=hT[:, k, i * P:(i + 1) * P], rhs=ring[sl][:, k, :], start=(k == 0), stop=(k == 7))
                            return ins
                        pg.op("pe", lambda e, i=i, bq=bq, sl_q=sl_q: mm(e, sl_q, i, bq), r=rring[sl_q] + [rhT[i]], w=[rPS[bq]])
                        pg.op("pe", lambda e, i=i, bk=bk, sl_k=sl_k: mm(e, sl_k, i, bk), r=rring[sl_k] + [rhT[i]], w=[rPS[bk]])
                        qk3 = qkf[:, :].rearrange("p (m d) -> p m d", m=16)
                        for half, bank in ((0, bq), (1, bk)):
                            evac("act", qkb[b2][:, half * 512:(half + 1) * 512], PSt[bank][:, :], [rPS[bank]], [rqkb[b2]])
                            evac("dve", qk3[:, half * 8:(half + 1) * 8, 0:16], PSt[bank][:, :].rearrange("p (m d) -> p m d", m=8)[:, :, 0:16], [rPS[bank], rqkb[b2]], rdtp[0:2], scale=1.0)
                        o3 = qkb[b2][:, :].rearrange("p (m d) -> p m d", m=16)
                        qk3 = qkf[:, :].rearrange("p (m d) -> p m d", m=16)
                        cb = cosT[:, ti, :].rearrange("p (m f) -> p m f", m=8)
                        sn = sinT[:, ti, :].rearrange("p (m f) -> p m f", m=8)
                        for half in range(0 if not os.environ.get("NOROPE") else 2, 2):
                            hs = slice(half * 8, (half + 1) * 8)
                            x1, x2 = qk3[:, hs, 0:8], qk3[:, hs, 8:16]
                            rr = rdtp[0:2] + [rcsx]
                            pg.op("dve", lambda e, x1=x1, cb=cb: e.tensor_tensor(out=rp[0][:, :, :], in0=x1, in1=cb, op=ALU.mult), r=rr, w=[rrp[0]])
                            pg.op("dve", lambda e, x2=x2, sn=sn: e.tensor_tensor(out=rp[1][:, :, :], in0=x2, in1=sn, op=ALU.mult), r=rr, w=[rrp[1]])
                            pg.op("dve", lambda e, o3=o3, hs=hs: e.tensor_tensor(out=o3[:, hs, 0:8], in0=rp[0][:, :, :], in1=rp[1][:, :, :], op=ALU.subtract), r=rrp[0:2], w=[rqkb[b2]])
                            pg.op("dve", lambda e, x2=x2, cb=cb: e.tensor_tensor(out=rp[2][:, :, :], in0=x2, in1=cb, op=ALU.mult), r=rr, w=[rrp[2]])
                            pg.op("dve", lambda e, x1=x1, sn=sn: e.tensor_tensor(out=rp[3][:, :, :], in0=x1, in1=sn, op=ALU.mult), r=rr, w=[rrp[3]])
                            pg.op("dve", lambda e, o3=o3, hs=hs: e.tensor_tensor(out=o3[:, hs, 8:16], in0=rp[2][:, :, :], in1=rp[3][:, :, :], op=ALU.add), r=rrp[2:4], w=[rqkb[b2]])
                        cur_tr = (i, ti, b2)
                    if pend_tr is not None:
                        ii, tii, bb = pend_tr
                        pb = 6 + tb[0] % 2
                        tb[0] += 1

                        def tr(e, pb=pb, bb=bb):
                            ins = None
                            for c in range(8):
                                ins = e.transpose(out=PSb[pb][:, c * P:(c + 1) * P], in_=qkb[bb][:, c * P:(c + 1) * P], identity=ident[:, :])
                            return ins
                        pg.op("pe", tr, r=[rqkb[bb], rconst], w=[rPS[pb]])
                        srcq = PSb[pb][:, 0:512].rearrange("p (h t) -> p h t", h=4)
                        srck = PSb[pb][:, 512:1024].rearrange("p (h t) -> p h t", h=4)
                        evac("act", qdT[:, :, ii * P:(ii + 1) * P], srcq, [rPS[pb]], [rqdT[hh][ii] for hh in range(4)])
                        evac("act", kdT[:, :, tii * P:(tii + 1) * P], srck, [rPS[pb]], [rkdT[hh][tii] for hh in range(4)])
                    pend_tr = cur_tr if i < TPG else None
                sl = next_fill()
                for i in range(TPG):
                    ti = q * TPG + i
                    bank = pjb[0] % 4
                    pjb[0] += 1

                    def mm(e, sl=sl, i=i, bank=bank):
                        ins = None
                        for k in range(8):
                            ins = e.matmul(PSt[bank][:, :], lhsT=hT[:, k, i * P:(i + 1) * P], rhs=ring[sl][:, k, :], start=(k == 0), stop=(k == 7))
                        return ins
                    pg.op("pe", mm, r=rring[sl] + [rhT[i]], w=[rPS[bank]])
                    evac("act", Vd[:, ti, :], PSt[bank][:, :], [rPS[bank]], [rVd[ti]])
                if LVL < 5:
                    continue
                nkt = q * TPG + TPG
                steps = [(h, j) for h in range(4) for j in range(nkt)]
                info = {}
                deferred = []
                T = dtp
                zb = 3

                def post_stage(stage, h):
                    if stage == 0:
                        pg.op("act", lambda e: e.activation(out=T[0][:, :], in_=PSt[5][:, :], func=AF.Ln), r=[rPS[5]], w=[rdtp[0]])
                        pg.op("dve", lambda e: e.tensor_scalar(out=T[1][:, :], in0=PSt[4][:, :], scalar1=1.0, scalar2=None, op0=ALU.mult), r=[rPS[4]], w=[rdtp[1]])
                        pg.op("act", lambda e: e.activation(out=T[2][:, :], in_=PSt[7][:, :], func=AF.Ln), r=[rPS[7]], w=[rdtp[2]])
                        pg.op("dve", lambda e: e.tensor_scalar(out=T[3][:, :], in0=PSt[6][:, :], scalar1=1.0, scalar2=None, op0=ALU.mult), r=[rPS[6]], w=[rdtp[3]])
                    elif stage == 1:
                        pg.op("act", lambda e: e.activation(out=T[0][:, :], in_=T[0][:, :], func=AF.Exp, scale=-1.0), r=[rdtp[0]], w=[rdtp[0]])
                        pg.op("act", lambda e: e.activation(out=T[2][:, :], in_=T[2][:, :], func=AF.Exp, scale=-1.0), r=[rdtp[2]], w=[rdtp[2]])
                        pg.op("dve", lambda e: e.tensor_tensor(out=T[1][:, :], in0=T[1][:, :], in1=T[0][:, :], op=ALU.mult), r=[rdtp[0], rdtp[1]], w=[rdtp[1]])
                        pg.op("dve", lambda e: e.tensor_tensor(out=T[3][:, :], in0=T[3][:, :], in1=T[2][:, :], op=ALU.mult), r=[rdtp[2], rdtp[3]], w=[rdtp[3]])
                        pg.op("dve", lambda e: e.scalar_tensor_tensor(out=T[1][:, :], in0=T[3][:, :], scalar=neglam, in1=T[1][:, :], op0=ALU.mult, op1=ALU.add), r=[rdtp[1], rdtp[3], rlam], w=[rdtp[1]])
                    elif stage == 2:
                        pg.op("act", lambda e: e.activation(out=sqb[:, :], in_=T[1][:, :], func=AF.Square), r=[rdtp[1]], w=[rsqb])
                    else:
                        pg.op("pe", lambda e: e.matmul(PSt[zb][:, :], lhsT=onesb[:, :], rhs=sqb[:, :], start=True, stop=True), r=[rsqb, rconst], w=[rPS[zb]])
                        pg.op("act", lambda e: e.activation(out=T[0][:, :], in_=PSt[zb][:, :], func=AF.Ln, scale=1.0 / 128.0, bias=epsb[:, 0:1]), r=[rPS[zb], repsb], w=[rdtp[0]])
                        pg.op("act", lambda e: e.activation(out=T[0][:, :], in_=T[0][:, :], func=AF.Exp, scale=-0.5), r=[rdtp[0]], w=[rdtp[0]])
                        pg.op("dve", lambda e, h=h: e.scalar_tensor_tensor(out=oTd[:, h, :], in0=T[1][:, :], scalar=gsub, in1=T[0][:, :], op0=ALU.mult, op1=ALU.mult), r=[rdtp[0], rdtp[1], rlam], w=[roTd[h]])

                for si in range(len(steps) + 1):
                    if si < len(steps):
                        h, j = steps[si]
                        n0 = max(j - q * TPG, 0)
                        diag = j >= q * TPG
                        s1 = (dsb[0] % 2) * 2
                        dsb[0] += 1
                        s2 = s1 + 1
                        p1 = (dpb[0] % 2) * 2
                        p2 = p1 + 1
                        dpb[0] += 1
                        qs = slice(n0 * P, G)

                        def smm(e, h=h, j=j, n0=n0, diag=diag, s1=s1, s2=s2, qs=qs):
                            ins = None
                            for mp, sbank in ((0, s1), (1, s2)):
                                rows = slice(mp * 64, (mp + 1) * 64)
                                if not diag:
                                    ins = e.matmul(PSt[sbank][:, qs], lhsT=kdT[rows, h, j * P:(j + 1) * P], rhs=qdT[rows, h, qs], start=True, stop=True)
                                    continue
                                dq_ = slice(n0 * P, (n0 + 1) * P)
                                e.matmul(PSt[sbank][:, dq_], lhsT=kdT[rows, h, j * P:(j + 1) * P], rhs=qdT[rows, h, dq_], start=True, stop=False)
                                ins = e.matmul(PSt[sbank][:, dq_], lhsT=ident[:, :], rhs=maskb[:, :], start=False, stop=True)
                                if (n0 + 1) * P < G:
                                    rq_ = slice((n0 + 1) * P, G)
                                    ins = e.matmul(PSt[sbank][:, rq_], lhsT=kdT[rows, h, j * P:(j + 1) * P], rhs=qdT[rows, h, rq_], start=True, stop=True, skip_group_check=True)
                            return ins
                        pg.op("pe", smm, r=[rkdT[h][j], rconst] + rqdT[h], w=[rPS[s1], rPS[s2]])
                        pg.op("act", lambda e, s1=s1, p1=p1, qs=qs: e.activation(out=Pball[:, p1:p1 + 2, qs], in_=PSall[:, s1:s1 + 2, qs], func=AF.Exp, scale=0.125),
                              r=[rPS[s1], rPS[s2]], w=[rPb[p1], rPb[p2]])
                        info[si] = (p1, p2, qs)
                    sj = si - 1
                    if sj >= 0:
                        hh, jj = steps[sj]
                        q1, q2, qss = info.pop(sj)

                        def pv(e, jj=jj, q1=q1, q2=q2, qss=qss, hh=hh):
                            st, sp_ = (jj == 0), (jj == nkt - 1)
                            e.matmul(PSt[4][:, qss], lhsT=Vd[:, jj, hh * P:(hh + 1) * P], rhs=Pb[q1][:, qss], start=st, stop=sp_)
                            e.matmul(PSt[5][:, qss], lhsT=onesb[:, :], rhs=Pb[q1][:, qss], start=st, stop=sp_)
                            e.matmul(PSt[6][:, qss], lhsT=Vd[:, jj, hh * P:(hh + 1) * P], rhs=Pb[q2][:, qss], start=st, stop=sp_)
                            return e.matmul(PSt[7][:, qss], lhsT=onesb[:, :], rhs=Pb[q2][:, qss], start=st, stop=sp_)
                        pg.op("pe", pv, r=[rVd[jj], rconst, rPb[q1], rPb[q2]], w=[rPS[4], rPS[5], rPS[6], rPS[7]])
                        if jj == nkt - 1:
                            post_stage(0, hh)
                            for st_ in (1, 2, 3):
                                deferred.append((si + st_, st_, hh))
                    while deferred and deferred[0][0] <= si:
                        _, st_, hh_ = deferred.pop(0)
                        post_stage(st_, hh_)
                for _, st_, hh_ in deferred:
                    post_stage(st_, hh_)
                if LVL < 6:
                    continue
                BPF = int(os.environ.get("BPF", "1"))
                if q + 1 < NG and BPF:
                    norm_pre(q + 1)
                if q == NG - 1:
                    xphase_pre()
                sl0 = next_fill()
                sl1 = next_fill(NR - 2)
                for i in range(TPG):
                    ti = q * TPG + i
                    b0 = (pjb[0] % 2) * 2
                    pjb[0] += 1

                    def wo(e, i=i, b0=b0, sl0=sl0, sl1=sl1, q=q):
                        ins = None
                        for k in range(8):
                            lt = oTf[:, k, q * G + i * P:q * G + (i + 1) * P] if k < 4 else oTd[:, k - 4, i * P:(i + 1) * P]
                            for nh, sl in ((0, sl0), (1, sl1)):
                                ins = e.matmul(PSt[b0 + nh][:, :], lhsT=lt, rhs=ring[sl][:, k, :], start=(k == 0), stop=(k == 7))
                        return ins
                    pg.op("pe", wo, r=rring[sl0] + rring[sl1] + [roTf[q]] + roTd, w=[rPS[b0], rPS[b0 + 1]])
                    post_update(ti, (b0, b0 + 1), 1.0)
                    if q + 1 < NG and ((BPF == 2 and i == 1) or (BPF == 1 and i == TPG - 1)):
                        norm_post(q + 1)
                        ndone.add(("B", seq, q + 1))
                    if q == NG - 1 and i == TPG - 1:
                        xphase_post((6, 7))

        for seq in range(NSEQ):
            def load_x(t0, t1, seq=seq):
                for ti in range(t0, t1):
                    if (seq, ti) in preloaded:
                        continue
                    row = seq * S + ti * P
                    dma("sp", X[:, ti, :], x_d[row:row + P, :], f"xs{ti}", w=[rX[ti]])
            load_x(0, TPG)
            late_x.append(lambda: load_x(TPG, NT))
            last_stage = stages[-1]
            if "ffn1" in stages:
                ffn_phase(seq, 0, last_stage == "ffn1")
            if "mix" in stages:
                mix_a(seq)
                mix_b(seq)
            if "ffn2" in stages:
                ffn_phase(seq, 1, last_stage == "ffn2")
            if last_stage == "mix":
                for ti in range(NT):
                    row = seq * S + ti * P
                    dma("sp", out_d[row:row + P, :], X[:, ti, :], f"xs{ti}", r=[rX[ti]])
        pg.finalize()

        sems = {}
        for n in sorted(sem_names) + ENGS:
            sems[n] = es.enter_context(nc.semaphore("s_" + n))
        print("n_sems", len(sems), {e: len(pg.q[e]) for e in ENGS})
        block = es.enter_context(nc.Block())
        store_events = [(o.ev) for o in pg.q["sp"] if o.dma is not None and o.dma.startswith("xs")]
        final = {}
        for (s_, v) in store_events:
            final[s_] = max(final.get(s_, 0), v)

        for eng in ENGS:
            def body(e, eng=eng):
                known = {}
                for o in pg.q[eng]:
                    need = {}
                    for d in o.deps:
                        s_, v = d.ev
                        if v > need.get(s_, 0):
                            need[s_] = v
                    for s_, v in need.items():
                        if known.get(s_, 0) >= v:
                            continue
                        e.wait_ge(sems[s_], v)
                        known[s_] = v
                    ins = o.fn(e)
                    if o.dma is not None:
                        ins.then_inc(sems[o.ev[0]], 16)
                    elif o.need:
                        ins.then_inc(sems[eng], 1)
                if eng == "sp":
                    for s_, v in final.items():
                        e.wait_ge(sems[s_], v)
            getattr(block, BLK[eng])(body)
    return nc


_CACHE = {}


def _consts():
    half = 8
    inv_freq = (500000.0 ** (-np.arange(0, 16, 2, dtype=np.float32) / np.float32(16))).astype(np.float32)
    ang = np.arange(S, dtype=np.float32)[:, None] * inv_freq[None, :]
    cs = np.stack([np.tile(np.cos(ang), (1, 8)), np.tile(np.sin(ang), (1, 8))]).astype(np.float32)
    ident = np.eye(P, dtype=np.float32)
    sidx = np.arange(P)[:, None]
    tidx = np.arange(P)[None, :]
    mask = np.where(sidx > tidx, np.float32(NEG), np.float32(0.0)).astype(np.float32)
    return cs, np.stack([ident, mask]).astype(np.float32)


def make_in_maps(inputs, n_cores=8):
    f = lambda a: np.ascontiguousarray(np.asarray(a, dtype=np.float32))
    x = f(inputs["x"])
    cs, cm = _consts()
    gains = np.concatenate([f(inputs[k]).reshape(1, D) for k in ("ffn1_pre_g", "ffn1_post_g", "mix_pre_g", "mix_post_g", "ffn2_pre_g", "ffn2_post_g")], axis=0)
    lamv = np.concatenate([f(inputs[k]).reshape(1, 64) for k in ("diff_lambda_q1", "diff_lambda_k1", "diff_lambda_q2", "diff_lambda_k2")], axis=0)
    shared = {
        "wg1": f(inputs["ffn1_w_gate"])[0], "wu1": f(inputs["ffn1_w_up"])[0], "wd1": f(inputs["ffn1_w_down"])[0],
        "wg2": f(inputs["ffn2_w_gate"])[0], "wu2": f(inputs["ffn2_w_up"])[0], "wd2": f(inputs["ffn2_w_down"])[0],
        "w_in": f(inputs["w_in"])[0], "w_out": f(inputs["w_out"])[0],
        "gains": gains, "fb": f(inputs["fox_forget_b"]).reshape(1, 8), "lamv": lamv,
        "subg": f(inputs["diff_subln_g"]).reshape(1, 128), "cossin": cs, "cmats": cm,
    }
    maps = []
    for c in range(n_cores):
        m = dict(shared)
        m["x"] = np.ascontiguousarray(x[c * NSEQ:(c + 1) * NSEQ].reshape(NSEQ * S, D))
        maps.append(m)
    return maps


def kernel(**inputs):
    if "nc" not in _CACHE:
        _CACHE["nc"] = build_program()
    nc = _CACHE["nc"]
    maps = make_in_maps(inputs, 8)
    res = run_bass_kernel_spmd(nc, maps, core_ids=list(range(8)))
    outs = [np.asarray(r["out"]).reshape(NSEQ, S, D) for r in res.results]
    return np.concatenate(outs, axis=0).astype(np.float32)
```
